# Optimizing a Trainium2 kernel written in Bass

```python
import jax, jax.numpy as jnp
from jax import lax
import numpy as np

D_MODEL = 1024
BATCH = 8
SEQ = 8192
DEPTH = 1
DEC_BATCH = 128
DEC_SEQ = 1
PAST_LEN = 8192
PAGE_SIZE = 128

HEAD_DIM = 64
D_MIX = D_MODEL
NSA_HEADS = D_MIX // (2 * HEAD_DIM)
NSA_KV_HEADS = 2
NSA_HPG = NSA_HEADS // NSA_KV_HEADS
RWKV_HEADS = D_MIX // (4 * HEAD_DIM)
MEM_HEADS = D_MIX // (4 * HEAD_DIM)
MEM_LEN = 256
NSA_W = NSA_HEADS * HEAD_DIM
KV_W = NSA_KV_HEADS * HEAD_DIM
RWKV_W = RWKV_HEADS * HEAD_DIM
MEM_W = MEM_HEADS * HEAD_DIM
CMP_BLK = 32
CMP_STRIDE = 16
CMP_HID = HEAD_DIM
SEL_BLK = 64
N_SEL = 16
WINDOW = 512
QBLK = 128
RWKV_LORA_W = 64
RWKV_LORA_A = 64
RWKV_SHIFT_W = 3 * RWKV_W + RWKV_LORA_W + RWKV_LORA_A
ROPE_THETA = 10000.0
RMS_EPS = 1e-6
GN_EPS = 64e-5
SCALE = HEAD_DIM ** -0.5
IN_SIZES = (NSA_W, 6 * KV_W, 3 * NSA_HEADS, NSA_W, RWKV_SHIFT_W, RWKV_W, MEM_W, MEM_W)
N_IN = sum(IN_SIZES)

kernel_name = 'nsa_rwkv7_memory_hybrid_step'


def rms_norm(x, g):
    xf = x.astype(jnp.float32)
    y = xf * lax.rsqrt(jnp.mean(xf * xf, axis=-1, keepdims=True) + RMS_EPS)
    return (y * g.astype(jnp.float32)).astype(x.dtype)


def rope(x, pos):
    half = HEAD_DIM // 2
    inv = ROPE_THETA ** (-2.0 * jnp.arange(half, dtype=jnp.float32) / HEAD_DIM)
    ang = pos.astype(jnp.float32)[:, None] * inv[None, :]
    cos = jnp.cos(ang)[:, None, :]
    sin = jnp.sin(ang)[:, None, :]
    xf = x.astype(jnp.float32)
    x1, x2 = xf[..., :half], xf[..., half:]
    return jnp.concatenate([x1 * cos - x2 * sin, x2 * cos + x1 * sin], axis=-1).astype(x.dtype)


def masked_softmax(s, mask):
    s = jnp.where(mask, s.astype(jnp.float32), -jnp.inf)
    m = jnp.max(s, axis=-1, keepdims=True)
    m = jnp.where(jnp.isfinite(m), m, 0.0)
    p = jnp.exp(s - m)
    return p / jnp.maximum(jnp.sum(p, axis=-1, keepdims=True), 1e-30)


def split_in(proj):
    cuts = [int(c) for c in np.cumsum(IN_SIZES)[:-1]]
    return jnp.split(proj, cuts, axis=-1)


def compress(rows2, pe, w1, b1, w2, b2):
    t = rows2.shape[1]
    n_ch = t // CMP_STRIDE
    ch = rows2[:, :n_ch * CMP_STRIDE].reshape(2, n_ch, CMP_STRIDE, NSA_KV_HEADS, HEAD_DIM)
    lo = jnp.einsum('ecpgd,epdh->ecgh', ch + pe[:, None, :CMP_STRIDE, None, :], w1[:, :CMP_STRIDE])
    hi = jnp.einsum('ecpgd,epdh->ecgh', ch + pe[:, None, CMP_STRIDE:, None, :], w1[:, CMP_STRIDE:])
    hid = jax.nn.silu(lo[:, :-1] + hi[:, 1:] + b1[:, None, None, :])
    return jnp.einsum('ecgh,ehd->ecgd', hid, w2) + b2[:, None, None, :]


def nsa_context(rows4, cmp_pe, cmp_w1, cmp_b1, cmp_w2, cmp_b2, ck_norm):
    t = rows4.shape[0]
    cmp2 = compress(jnp.moveaxis(rows4[:, :2], 1, 0), cmp_pe, cmp_w1, cmp_b1, cmp_w2, cmp_b2)
    n_cb = cmp2.shape[1]
    cend = jnp.arange(n_cb) * CMP_STRIDE + CMP_BLK - 1
    ck = rope(rms_norm(cmp2[0], ck_norm), cend)
    cv = cmp2[1]
    n_sb = -(-t // SEL_BLK)
    sel = jnp.pad(rows4[:, 2:], ((0, n_sb * SEL_BLK - t), (0, 0), (0, 0), (0, 0)))
    sel = sel.reshape(n_sb, SEL_BLK, 2, NSA_KV_HEADS, HEAD_DIM).transpose(2, 3, 0, 1, 4)
    cstart = jnp.arange(n_cb) * CMP_STRIDE
    sstart = jnp.arange(n_sb) * SEL_BLK
    ov = jnp.clip(jnp.minimum(cstart[:, None] + CMP_BLK, sstart[None, :] + SEL_BLK)
                  - jnp.maximum(cstart[:, None], sstart[None, :]), 0)
    ov = ov.astype(jnp.float32) / CMP_BLK
    return ck, cv, cend, sel[0], sel[1], ov


def cmp_sel_attend(q, qpos, ck, cv, cend, ksb, vsb, ov):
    nq = q.shape[0]
    qg = q.reshape(nq, NSA_KV_HEADS, NSA_HPG, HEAD_DIM)
    s = jnp.einsum('qghd,cgd->qghc', qg, ck) * SCALE
    valid = cend[None, :] <= qpos[:, None]
    p = masked_softmax(s, valid[:, None, None, :])
    o_c = jnp.einsum('qghc,cgd->qghd', p.astype(cv.dtype), cv)
    imp = jnp.einsum('qghc,cj->qgj', p, ov)
    n_sb = ksb.shape[1]
    jj = jnp.arange(n_sb)
    cur = qpos // SEL_BLK
    forced = (jj[None, :] == 0) | (jj[None, :] == cur[:, None]) | (jj[None, :] == cur[:, None] - 1)
    causal = jj[None, :] * SEL_BLK <= qpos[:, None]
    score = jnp.where(causal[:, None, :], jnp.where(forced[:, None, :], jnp.inf, imp), -jnp.inf)
    n_sel = min(N_SEL, n_sb)
    _, idx = lax.top_k(score, n_sel)
    gidx = jnp.arange(NSA_KV_HEADS)[None, :, None]
    kg = ksb[gidx, idx]
    vg = vsb[gidx, idx]
    ss = jnp.einsum('qghd,qgnsd->qghns', qg, kg) * SCALE
    kp = idx[..., None] * SEL_BLK + jnp.arange(SEL_BLK)
    km = kp <= qpos[:, None, None, None]
    ps = masked_softmax(ss.reshape(nq, NSA_KV_HEADS, NSA_HPG, n_sel * SEL_BLK),
                        km.reshape(nq, NSA_KV_HEADS, 1, n_sel * SEL_BLK))
    o_s = jnp.einsum('qghk,qgkd->qghd', ps.astype(vg.dtype),
                     vg.reshape(nq, NSA_KV_HEADS, n_sel * SEL_BLK, HEAD_DIM))
    return o_c.reshape(nq, NSA_HEADS, HEAD_DIM), o_s.reshape(nq, NSA_HEADS, HEAD_DIM)


def window_attend(q, qpos, kw, vw, kpos):
    nq = q.shape[0]
    qg = q.reshape(nq, NSA_KV_HEADS, NSA_HPG, HEAD_DIM)
    s = jnp.einsum('qghd,kgd->qghk', qg, kw) * SCALE
    mask = (kpos[None, :] <= qpos[:, None]) & (kpos[None, :] > qpos[:, None] - WINDOW) & (kpos[None, :] >= 0)
    p = masked_softmax(s, mask[:, None, None, :])
    o = jnp.einsum('qghk,kgd->qghd', p.astype(vw.dtype), vw)
    return o.reshape(nq, NSA_HEADS, HEAD_DIM)


def gate_mix(g, o_c, o_s, o_w):
    return g[..., 0:1] * o_c + g[..., 1:2] * o_s + g[..., 2:3] * o_w


def nsa_prompt_seq(q, rows4, rows_w, gates, cmp_pe, cmp_w1, cmp_b1, cmp_w2, cmp_b2, ck_norm):
    t = q.shape[0]
    ck, cv, cend, ksb, vsb, ov = nsa_context(rows4, cmp_pe, cmp_w1, cmp_b1, cmp_w2, cmp_b2, ck_norm)
    wpad = jnp.pad(rows_w, ((WINDOW, 0), (0, 0), (0, 0), (0, 0)))

    def one_block(start):
        qb = lax.dynamic_slice_in_dim(q, start, QBLK)
        gb = lax.dynamic_slice_in_dim(gates, start, QBLK)
        qpos = start + jnp.arange(QBLK)
        o_c, o_s = cmp_sel_attend(qb, qpos, ck, cv, cend, ksb, vsb, ov)
        wb = lax.dynamic_slice_in_dim(wpad, start, QBLK + WINDOW)
        kpos = start - WINDOW + jnp.arange(QBLK + WINDOW)
        o_w = window_attend(qb, qpos, wb[:, 0], wb[:, 1], kpos)
        return gate_mix(gb, o_c, o_s, o_w)

    out = lax.map(one_block, jnp.arange(t // QBLK) * QBLK)
    return out.reshape(t, NSA_HEADS, HEAD_DIM)


def nsa_decode_seq(q, pt_row, rows4_new, rows_w_new, win_buf, gates, pool,
                   cmp_pe, cmp_w1, cmp_b1, cmp_w2, cmp_b2, ck_norm):
    past = pool[pt_row].reshape(-1, 4, NSA_KV_HEADS, HEAD_DIM)
    p_len = past.shape[0]
    s_len = q.shape[0]
    rows4 = jnp.concatenate([past, rows4_new], axis=0)
    ck, cv, cend, ksb, vsb, ov = nsa_context(rows4, cmp_pe, cmp_w1, cmp_b1, cmp_w2, cmp_b2, ck_norm)
    qpos = p_len + jnp.arange(s_len)
    o_c, o_s = cmp_sel_attend(q, qpos, ck, cv, cend, ksb, vsb, ov)
    nb = win_buf.shape[0]
    kw = jnp.concatenate([win_buf, rows_w_new], axis=0)
    kpos = p_len - nb + jnp.arange(nb + s_len)
    o_w = window_attend(q, qpos, kw[:, 0], kw[:, 1], kpos)
    return gate_mix(gates, o_c, o_s, o_w), kw[s_len:]


def nsa_project(q_raw, kv_raw, pos, q_norm, k_norm):
    b, t = q_raw.shape[:2]
    q = rope(rms_norm(q_raw.reshape(b, t, NSA_HEADS, HEAD_DIM), q_norm), pos)
    kv = kv_raw.reshape(b, t, 6, NSA_KV_HEADS, HEAD_DIM)
    k_sel = rope(rms_norm(kv[:, :, 2], k_norm[1]), pos)
    k_win = rope(rms_norm(kv[:, :, 4], k_norm[2]), pos)
    rows4 = jnp.stack([kv[:, :, 0], kv[:, :, 1], k_sel, kv[:, :, 3]], axis=2)
    rows_w = jnp.stack([k_win, kv[:, :, 5]], axis=2)
    return q, rows4, rows_w


def rwkv7_mix(u_cur, prev_row, s0, mu, w0, w2, a0, a2, k_k, k_a, r_k, ln_w, ln_b):
    b, t = u_cur.shape[:2]
    prev = jnp.concatenate([prev_row[:, None], u_cur[:, :-1]], axis=1)
    u = u_cur + (prev - u_cur) * mu
    r, k, v, wl, al = jnp.split(u, [RWKV_W, 2 * RWKV_W, 3 * RWKV_W, 3 * RWKV_W + RWKV_LORA_W], axis=-1)
    w = -jax.nn.softplus(-(w0 + jnp.tanh(wl) @ w2)) - 0.5
    decay = jnp.exp(-jnp.exp(w.astype(jnp.float32)))
    a = jax.nn.sigmoid(a0 + al @ a2)

    def heads(z):
        return z.reshape(b, t, RWKV_HEADS, HEAD_DIM).astype(jnp.float32)

    r, k, v, a, decay = heads(r), heads(k), heads(v), heads(a), heads(decay)
    kk = k * k_k.reshape(RWKV_HEADS, HEAD_DIM).astype(jnp.float32)
    kk = kk / jnp.maximum(jnp.sqrt(jnp.sum(kk * kk, axis=-1, keepdims=True)), 1e-12)
    k = k * (1.0 + (a - 1.0) * k_a.reshape(RWKV_HEADS, HEAD_DIM).astype(jnp.float32))

    def step(s, inp):
        r_t, k_t, v_t, w_t, kk_t, b_t = inp
        sa = jnp.einsum('bhvk,bhk->bhv', s, -kk_t)
        s = s * w_t[:, :, None, :] + sa[..., None] * b_t[:, :, None, :] + v_t[..., None] * k_t[:, :, None, :]
        return s, jnp.einsum('bhvk,bhk->bhv', s, r_t)

    xs = tuple(jnp.moveaxis(z, 1, 0) for z in (r, k, v, decay, kk, kk * a))
    s_fin, ys = lax.scan(step, s0.astype(jnp.float32), xs)
    y = jnp.moveaxis(ys, 0, 1)
    mean = jnp.mean(y, axis=-1, keepdims=True)
    var = jnp.mean(jnp.square(y - mean), axis=-1, keepdims=True)
    y = ((y - mean) * lax.rsqrt(var + GN_EPS)).reshape(b, t, RWKV_W)
    y = y * ln_w.astype(jnp.float32) + ln_b.astype(jnp.float32)
    bonus = jnp.sum(r * k * r_k.astype(jnp.float32), axis=-1, keepdims=True) * v
    y = y + bonus.reshape(b, t, RWKV_W)
    return y.astype(u_cur.dtype), u_cur[:, -1], s_fin.astype(s0.dtype)


def mem_kv(mem, g, w_kv, k_norm):
    b = mem.shape[0]
    kv = (rms_norm(mem, g) @ w_kv).reshape(b, mem.shape[1], 2, MEM_HEADS, HEAD_DIM)
    return jnp.stack([rms_norm(kv[:, :, 0], k_norm), kv[:, :, 1]], axis=2)


def mem_attend(q_raw, mkv, q_norm):
    b, t = q_raw.shape[:2]
    q = rms_norm(q_raw.reshape(b, t, MEM_HEADS, HEAD_DIM), q_norm)
    s = jnp.einsum('bthd,bmhd->bhtm', q, mkv[:, :, 0]) * SCALE
    p = jax.nn.softmax(s.astype(jnp.float32), axis=-1)
    return jnp.einsum('bhtm,bmhd->bthd', p.astype(mkv.dtype), mkv[:, :, 1])


def mixer_out(o_nsa, z_nsa, o_rw, z_rw, o_mem, z_mem, w_out):
    b, t = z_nsa.shape[:2]
    cat = jnp.concatenate([o_nsa.reshape(b, t, NSA_W) * jax.nn.silu(z_nsa),
                           o_rw * jax.nn.silu(z_rw),
                           o_mem.reshape(b, t, MEM_W) * jax.nn.silu(z_mem)], axis=-1)
    return cat @ w_out


def setup_inputs(seed: int = 0) -> dict:
    key = jax.random.key(seed)
    ks = iter(jax.random.split(key, 48))

    def nrm(shape, scale):
        return scale * jax.random.normal(next(ks), shape, jnp.float32)

    def gain(shape):
        return 1.0 + nrm(shape, 0.05)

    n_pages = PAST_LEN // PAGE_SIZE
    n_used = DEC_BATCH * n_pages
    n_pool = n_used + n_used // 4
    win_buf = min(WINDOW, PAST_LEN)
    page_table = jax.random.permutation(next(ks), n_pool)[:n_used].reshape(DEC_BATCH, n_pages).astype(jnp.int32)
    return {
        'x_prompt': nrm((BATCH, SEQ, D_MODEL), 1.0),
        'x_sample': nrm((DEC_BATCH, DEC_SEQ, D_MODEL), 1.0),
        'mem_prompt': nrm((BATCH, MEM_LEN, D_MODEL), 1.0),
        'cache_nsa': nrm((DEPTH, n_pool, PAGE_SIZE, 4, NSA_KV_HEADS, HEAD_DIM), 1.0),
        'cache_win': nrm((DEPTH, DEC_BATCH, win_buf, 2, NSA_KV_HEADS, HEAD_DIM), 1.0),
        'cache_mem': nrm((DEPTH, DEC_BATCH, MEM_LEN, 2, MEM_HEADS, HEAD_DIM), 1.0),
        'state_rwkv_shift': nrm((DEPTH, DEC_BATCH, RWKV_SHIFT_W), 1.0),
        'state_rwkv_wkv': nrm((DEPTH, DEC_BATCH, RWKV_HEADS, HEAD_DIM, HEAD_DIM), 0.3),
        'page_table': page_table,
        'ln_g': gain((DEPTH, D_MODEL)),
        'w_in': nrm((DEPTH, D_MODEL, N_IN), D_MODEL ** -0.5),
        'nsa_q_norm': gain((DEPTH, HEAD_DIM)),
        'nsa_k_norm': gain((DEPTH, 3, HEAD_DIM)),
        'cmp_pe': nrm((DEPTH, 2, CMP_BLK, HEAD_DIM), 0.1),
        'cmp_w1': nrm((DEPTH, 2, CMP_BLK, HEAD_DIM, CMP_HID), (CMP_BLK * HEAD_DIM) ** -0.5),
        'cmp_b1': nrm((DEPTH, 2, CMP_HID), 0.02),
        'cmp_w2': nrm((DEPTH, 2, CMP_HID, HEAD_DIM), CMP_HID ** -0.5),
        'cmp_b2': nrm((DEPTH, 2, HEAD_DIM), 0.02),
        'rwkv_mu': jax.random.uniform(next(ks), (DEPTH, RWKV_SHIFT_W), jnp.float32),
        'rwkv_w0': jax.random.uniform(next(ks), (DEPTH, RWKV_W), jnp.float32, -6.0, 0.0),
        'rwkv_w2': nrm((DEPTH, RWKV_LORA_W, RWKV_W), 0.5 * RWKV_LORA_W ** -0.5),
        'rwkv_a0': nrm((DEPTH, RWKV_W), 0.1),
        'rwkv_a2': nrm((DEPTH, RWKV_LORA_A, RWKV_W), 0.5 * RWKV_LORA_A ** -0.5),
        'rwkv_k_k': 0.85 + nrm((DEPTH, RWKV_W), 0.05),
        'rwkv_k_a': gain((DEPTH, RWKV_W)),
        'rwkv_r_k': nrm((DEPTH, RWKV_HEADS, HEAD_DIM), 0.1),
        'rwkv_ln_w': gain((DEPTH, RWKV_W)),
        'rwkv_ln_b': nrm((DEPTH, RWKV_W), 0.02),
        'mem_norm_g': gain((DEPTH, D_MODEL)),
        'w_mem_kv': nrm((DEPTH, D_MODEL, 2 * MEM_W), D_MODEL ** -0.5),
        'mem_q_norm': gain((DEPTH, HEAD_DIM)),
        'mem_k_norm': gain((DEPTH, HEAD_DIM)),
        'w_out': nrm((DEPTH, D_MIX, D_MODEL), D_MIX ** -0.5),
    }


def reference(x_prompt, x_sample, mem_prompt, cache_nsa, cache_win, cache_mem,
              state_rwkv_shift, state_rwkv_wkv, page_table,
              ln_g, w_in, nsa_q_norm, nsa_k_norm, cmp_pe, cmp_w1, cmp_b1, cmp_w2, cmp_b2,
              rwkv_mu, rwkv_w0, rwkv_w2, rwkv_a0, rwkv_a2, rwkv_k_k, rwkv_k_a, rwkv_r_k,
              rwkv_ln_w, rwkv_ln_b, mem_norm_g, w_mem_kv, mem_q_norm, mem_k_norm, w_out):
    bp, tp = x_prompt.shape[:2]
    bs, ts = x_sample.shape[:2]
    pos_p = jnp.arange(tp)
    pos_s = PAST_LEN + jnp.arange(ts)
    win_p = min(WINDOW, tp)
    hp, hs = x_prompt, x_sample
    nsa_p, win_pl, shift_pl, wkv_pl, memkv_pl = [], [], [], [], []
    nsa_sl, win_sl, shift_sl, wkv_sl = [], [], [], []
    for l in range(DEPTH):
        cmp_params = (cmp_pe[l], cmp_w1[l], cmp_b1[l], cmp_w2[l], cmp_b2[l], nsa_k_norm[l, 0])
        rwkv_params = (rwkv_mu[l], rwkv_w0[l], rwkv_w2[l], rwkv_a0[l], rwkv_a2[l], rwkv_k_k[l],
                       rwkv_k_a[l], rwkv_r_k[l], rwkv_ln_w[l], rwkv_ln_b[l])

        q_raw, kv_raw, g_raw, z_nsa, u_rw, z_rw, q_mem, z_mem = split_in(rms_norm(hp, ln_g[l]) @ w_in[l])
        q, rows4, rows_w = nsa_project(q_raw, kv_raw, pos_p, nsa_q_norm[l], nsa_k_norm[l])
        gates = jax.nn.sigmoid(g_raw).reshape(bp, tp, NSA_HEADS, 3)
        o_nsa = lax.map(lambda a: nsa_prompt_seq(*a, *cmp_params), (q, rows4, rows_w, gates))
        o_rw, sh_p, st_p = rwkv7_mix(u_rw, jnp.zeros((bp, RWKV_SHIFT_W), u_rw.dtype),
                                     jnp.zeros((bp, RWKV_HEADS, HEAD_DIM, HEAD_DIM), hp.dtype), *rwkv_params)
        mkv_p = mem_kv(mem_prompt, mem_norm_g[l], w_mem_kv[l], mem_k_norm[l])
        o_mem = mem_attend(q_mem, mkv_p, mem_q_norm[l])
        hp = hp + mixer_out(o_nsa, z_nsa, o_rw, z_rw, o_mem, z_mem, w_out[l])
        nsa_p.append(rows4)
        win_pl.append(rows_w[:, tp - win_p:])
        shift_pl.append(sh_p)
        wkv_pl.append(st_p)
        memkv_pl.append(mkv_p)

        q_raw, kv_raw, g_raw, z_nsa, u_rw, z_rw, q_mem, z_mem = split_in(rms_norm(hs, ln_g[l]) @ w_in[l])
        q, rows4, rows_w = nsa_project(q_raw, kv_raw, pos_s, nsa_q_norm[l], nsa_k_norm[l])
        gates = jax.nn.sigmoid(g_raw).reshape(bs, ts, NSA_HEADS, 3)
        pool = cache_nsa[l]
        o_nsa, win_new = lax.map(lambda a: nsa_decode_seq(*a, pool, *cmp_params),
                                 (q, page_table, rows4, rows_w, cache_win[l], gates))
        o_rw, sh_s, st_s = rwkv7_mix(u_rw, state_rwkv_shift[l], state_rwkv_wkv[l], *rwkv_params)
        o_mem = mem_attend(q_mem, cache_mem[l], mem_q_norm[l])
        hs = hs + mixer_out(o_nsa, z_nsa, o_rw, z_rw, o_mem, z_mem, w_out[l])
        nsa_sl.append(rows4)
        win_sl.append(win_new)
        shift_sl.append(sh_s)
        wkv_sl.append(st_s)

    nsa_rows_prompt = jnp.stack(nsa_p)
    win_prompt = jnp.stack(win_pl)
    shift_prompt = jnp.stack(shift_pl)
    wkv_prompt = jnp.stack(wkv_pl)
    mem_kv_prompt = jnp.stack(memkv_pl)
    nsa_rows_sample = jnp.stack(nsa_sl)
    win_sample = jnp.stack(win_sl)
    shift_sample = jnp.stack(shift_sl)
    wkv_sample = jnp.stack(wkv_sl)
    return (hp, hs, nsa_rows_prompt, win_prompt, shift_prompt, wkv_prompt, mem_kv_prompt,
            nsa_rows_sample, win_sample, shift_sample, wkv_sample)
```

```python
import contextlib
import math
import numpy as np
import concourse.bass as bass
import concourse.mybir as mybir
from concourse.bass_utils import run_bass_kernel_spmd

F32 = mybir.dt.float32
BF16 = mybir.dt.bfloat16
I32 = mybir.dt.int32
AF = mybir.ActivationFunctionType
ALU = mybir.AluOpType
AX = mybir.AxisListType

NCORES = 8
D = 1024
SEQ = 8192
NT = SEQ // 128
N_IN = 3480
DB = 16
C_Q, C_KV, C_G, C_ZN, C_U, C_ZR, C_QM, C_ZM = 0, 512, 1280, 1304, 1816, 2712, 2968, 3224
RMS_EPS = 1e-6
ENGS = ("sync", "scalar", "vector", "gpsimd", "tensor")
SEM_ROT = 30000
OUTQ = "sync"
import os as _os
YP_ON = _os.environ.get("YP_ON", "1") == "1"
RWKV_ON = _os.environ.get("RWKV_ON", "1") == "1"


class Sched:
    def __init__(self, nc, stack):
        self.nc = nc
        self.stack = stack
        self.streams = {e: [] for e in ENGS}
        self.sem = {}
        self.cnt = {}
        self.nsem = 0
        for e in ENGS:
            self._new_eng_sem(e)
        self.known = {e: {} for e in ENGS}
        self.last_w = {}
        self.readers = {}
        self.dsem = {}
        self.out_events = []

    def _alloc(self, name):
        self.nsem += 1
        return self.stack.enter_context(self.nc.semaphore(f"sm{self.nsem}"))

    def _new_eng_sem(self, e):
        self.sem[e] = self._alloc("e_" + e)
        self.cnt[e] = 0

    def _collect(self, E, reads, writes):
        waits = {}

        def need(ev):
            if ev is None:
                return
            s, v = ev
            cur = waits.get(s.num)
            if cur is None or cur[1] < v:
                waits[s.num] = (s, v)

        for k in reads:
            need(self.last_w.get(k))
        for k in writes:
            need(self.last_w.get(k))
            for ev in self.readers.get(k, {}).values():
                need(ev)
        out = []
        for num, (s, v) in waits.items():
            if E == "tensor" and s.num == self.sem["tensor"].num:
                continue
            if self.known[E].get(num, 0) >= v:
                continue
            self.known[E][num] = v
            out.append((s, v))
        return out

    def _commit(self, ev, reads, writes):
        for k in writes:
            self.last_w[k] = ev
            self.readers[k] = {}
        for k in reads:
            d = self.readers.setdefault(k, {})
            cur = d.get(ev[0].num)
            if cur is None or cur[1] < ev[1]:
                d[ev[0].num] = ev

    def op(self, E, fn, reads=(), writes=()):
        waits = self._collect(E, reads, writes)
        if self.cnt[E] >= SEM_ROT:
            self._new_eng_sem(E)
        self.cnt[E] += 1
        s = self.sem[E]
        ev = (s, self.cnt[E])

        def emit(eng, fn=fn, waits=waits, s=s):
            for (ws, wv) in waits:
                eng.wait_ge(ws, wv)
            fn(eng).then_inc(s, 1)

        self.streams[E].append(emit)
        self._commit(ev, reads, writes)
        return ev

    def dma(self, E, fn, group, reads=(), writes=(), is_out=False):
        waits = self._collect(E, reads, writes)
        g = self.dsem.get(group)
        if g is None or g[1] >= SEM_ROT:
            g = [self._alloc("d_" + str(group)), 0]
            self.dsem[group] = g
        g[1] += 16
        s = g[0]
        ev = (s, g[1])

        def emit(eng, fn=fn, waits=waits, s=s):
            for (ws, wv) in waits:
                eng.wait_ge(ws, wv)
            fn(eng).then_inc(s, 16)

        self.streams[E].append(emit)
        self._commit(ev, reads, writes)
        if is_out:
            self.out_events.append(ev)
        return ev

    def barrier(self):
        evs = [(self.sem[e], self.cnt[e]) for e in ENGS if self.cnt[e] > 0]
        evs += [(g[0], g[1]) for g in self.dsem.values()]
        for E in ENGS:
            waits = []
            for (s_, v) in evs:
                if s_.num == self.sem[E].num:
                    continue
                if self.known[E].get(s_.num, 0) >= v:
                    continue
                self.known[E][s_.num] = v
                waits.append((s_, v))

            def emit(eng, waits=waits):
                for (ws, wv) in waits:
                    eng.wait_ge(ws, wv)

            self.streams[E].append(emit)

    def finish(self, E="sync"):
        fin = {}
        for (s, v) in self.out_events:
            if fin.get(s.num, (s, 0))[1] < v:
                fin[s.num] = (s, v)
        lst = list(fin.values())

        def emit(eng):
            for (s, v) in lst:
                eng.wait_ge(s, v)

        self.streams[E].append(emit)

    def run(self, block):
        streams = self.streams

        @block.sync
        def _(e):
            for f in streams["sync"]:
                f(e)

        @block.scalar
        def _(e):
            for f in streams["scalar"]:
                f(e)

        @block.vector
        def _(e):
            for f in streams["vector"]:
                f(e)

        @block.gpsimd
        def _(e):
            for f in streams["gpsimd"]:
                f(e)

        @block.tensor
        def _(e):
            for f in streams["tensor"]:
                f(e)


def build_nc(stage=99, NPOOL_ROWS=10240 * 128, nseq=DB, NT=NT):
    nc = bass.Bass("TRN2", target_bir_lowering=False)

    def din(name, shape, dt=F32):
        return nc.dram_tensor(name, list(shape), dt, kind="ExternalInput").ap()

    def dout(name, shape, dt=F32):
        return nc.dram_tensor(name, list(shape), dt, kind="ExternalOutput").ap()

    xp = din("xp", [SEQ, D])
    xs = din("xs", [DB, D])
    memp = din("memp", [256, D])
    cwin = din("cwin", [DB, 512, 256])
    st_shift = din("st_shift", [DB, 896])
    st_wkv = din("st_wkv", [DB, 4, 64, 64])
    ln_g = din("ln_g", [D])
    w_in = din("w_in", [D, N_IN])
    nsa_k_norm = din("nsa_k_norm", [3, 64])
    mem_norm_g = din("mem_norm_g", [D])
    w_mem_kv = din("w_mem_kv", [D, 512])
    mem_k_norm = din("mem_k_norm", [64])
    rwkv_mu = din("rwkv_mu", [896])
    rwkv_w0 = din("rwkv_w0", [256])
    rwkv_w2 = din("rwkv_w2", [64, 256])
    rwkv_a0 = din("rwkv_a0", [256])
    rwkv_a2 = din("rwkv_a2", [64, 256])
    rwkv_k_k = din("rwkv_k_k", [256])
    rwkv_k_a = din("rwkv_k_a", [256])
    rwkv_r_k = din("rwkv_r_k", [256])
    rwkv_ln_w = din("rwkv_ln_w", [256])
    rwkv_ln_b = din("rwkv_ln_b", [256])
    nsa_q_norm = din("nsa_q_norm", [64])
    mem_q_norm = din("mem_q_norm", [64])
    w_out = din("w_out", [D, D])
    cmem = din("cmem", [DB, 256, 512])
    c_onehot = din("c_onehot", [16, 16, 128])
    pool = din("pool", [NPOOL_ROWS, 512])
    ptab = din("ptab", [DB * 64], I32)
    cmp_pe = din("cmp_pe", [2, 32, 64])
    cmp_w1 = din("cmp_w1", [2, 32, 64, 64])
    cmp_b1 = din("cmp_b1", [2, 64])
    cmp_w2 = din("cmp_w2", [2, 64, 64])
    cmp_b2 = din("cmp_b2", [2, 64])
    c_ov = din("c_ov", [128, 4, 128])
    c_cosc = din("c_cosc", [128, 4, 64])
    c_valid = din("c_valid", [128, 8])
    c_gsel = din("c_gsel", [2, 8])
    c_lsel = din("c_lsel", [2, 2, 128])
    c_iota = din("c_iota", [128, 1])
    c_shift = din("c_shift", [128, 2, 128])
    c_masks = din("c_masks", [128, 19, 128])
    c_cos = din("c_cos", [128, NT, 32])
    c_sin = din("c_sin", [128, NT, 32])
    c_cosd = din("c_cosd", [64])
    c_ident = din("c_ident", [128, 128])

    o_yp = dout("o_yp", [SEQ, D])
    o_ys = dout("o_ys", [DB, D])
    o_nsa_p = dout("o_nsa_p", [SEQ, 512])
    o_win_p = dout("o_win_p", [512, 256])
    o_shift_p = dout("o_shift_p", [1, 896])
    o_wkv_p = dout("o_wkv_p", [4, 64, 64])
    o_memkv = dout("o_memkv", [256, 512])
    o_nsa_s = dout("o_nsa_s", [DB, 512])
    o_win_s = dout("o_win_s", [DB, 512, 256])
    o_shift_s = dout("o_shift_s", [DB, 896])
    o_wkv_s = dout("o_wkv_s", [DB, 4, 64, 64])
    scr_ps = nc.dram_tensor("scr_ps", [SEQ, 2080], F32, kind="Internal").ap()
    scr_win = nc.dram_tensor("scr_win", [SEQ, 256], F32, kind="Internal").ap()
    scr_c = nc.dram_tensor("scr_c", [DB, 8, 128], F32, kind="Internal").ap()
    scr_s = nc.dram_tensor("scr_s", [DB, 8, 129], F32, kind="Internal").ap()
    scr_w = nc.dram_tensor("scr_w", [DB, 8, 129], F32, kind="Internal").ap()
    scr_m = nc.dram_tensor("scr_m", [DB, 4, 257], F32, kind="Internal").ap()

    with contextlib.ExitStack() as stack:
        def sb(name, shape, dt=F32):
            return stack.enter_context(nc.sbuf_tensor(name, list(shape), dt))

        def ps(name):
            return stack.enter_context(nc.psum_tensor(name, [128, 512], F32))

        S = Sched(nc, stack)

        gT = sb("gT", [128, 8], F32)
        cosd = sb("cosd", [16, 64], F32)
        identF = sb("identF", [128, 128], F32)
        identB = sb("identB", [128, 128], BF16)
        gainK = sb("gainK", [128, 4, 64], F32)
        epsT = sb("epsT", [128, 1], F32)
        rows4 = [sb(f"rows4_{i}", [128, 512], F32) for i in range(2)]
        winr = [sb(f"winr{i}", [128, 256], F32) for i in range(2)]
        kraw = [sb(f"kraw{i}", [128, 8, 64], F32) for i in range(2)]
        ksq = sb("ksq", [128, 8, 64], F32)
        kss = sb("kss", [128, 8], F32)
        krs = sb("krs", [128, 8], F32)
        kn = sb("kn", [128, 8, 64], F32)
        kro = sb("kro", [128, 8, 64], F32)
        ktmp = sb("ktmp", [128, 8, 32], F32)
        dproj = sb("dproj", [16, N_IN], F32)
        xs_sb = sb("xs_sb", [16, D], F32)
        p1 = contextlib.ExitStack()

        def sb1(name, shape, dt=F32):
            return p1.enter_context(nc.sbuf_tensor(name, list(shape), dt))

        Wbf = sb1("Wbf", [128, 8, N_IN], BF16)
        Wmk = sb1("Wmk", [128, 8, 512], BF16)
        gmT = sb1("gmT", [128, 8], F32)
        cosT = sb1("cosT", [128, NT, 32], F32)
        sinT = sb1("sinT", [128, NT, 32], F32)
        gainM = sb1("gainM", [128, 4, 64], F32)
        p0 = contextlib.ExitStack()
        wstage = [p0.enter_context(nc.sbuf_tensor(f"wstage{i}", [128, N_IN], F32)) for i in range(2)]

        psT = ps("psT")
        psA = [ps("psA0"), ps("psA1")]
        psB = [ps("psB0"), ps("psB1")]
        psC = ps("psC")

        def ld(dst, src, key, eng="sync", group=None, slow=False, shared=True):
            S.dma(eng, lambda e: e.dma_start(out=dst, in_=src, allow_slow_non_contiguous=slow),
                  "setup" if shared else (group or key), writes=[key])

        ld(gT[:], ln_g.rearrange("(k p) -> p k", p=128), "gT", slow=True)
        ld(gmT[:], mem_norm_g.rearrange("(k p) -> p k", p=128), "gmT", slow=True)
        ld(cosT[:], c_cos, "cosT")
        ld(sinT[:], c_sin, "sinT")
        ld(cosd[:], c_cosd.partition_broadcast(16), "cosd")
        ld(identF[:], c_ident, "identF")
        for j in range(4):
            src = nsa_k_norm[1 if j < 2 else 2]
            ld(gainK[:, j, :], src.partition_broadcast(128), ("gainK", j))
            ld(gainM[:, j, :], mem_k_norm.partition_broadcast(128), ("gainM", j))
        S.barrier()
        S.op("vector", lambda e: e.memset(epsT[:], RMS_EPS), writes=["epsT"])
        S.op("vector", lambda e: e.tensor_copy(out=identB[:], in_=identF[:]),
             reads=["identF"], writes=["identB"])
        for k in range(8):
            b = k % 2
            ld(wstage[b][:], w_in[k * 128:(k + 1) * 128, :], ("wstage", b), shared=False)
            eng = "gpsimd" if k % 2 == 0 else "vector"
            S.op(eng, lambda e, k=k, b=b: e.tensor_copy(out=Wbf[:, k, :], in_=wstage[b][:]),
                 reads=[("wstage", b)], writes=[("Wbf", k)])
        for k in range(8):
            b = k % 2
            ld(wstage[b][:, 0:512], w_mem_kv[k * 128:(k + 1) * 128, :], ("wstage", b), shared=False)
            eng = "gpsimd" if k % 2 == 0 else "vector"
            S.op(eng, lambda e, k=k, b=b: e.tensor_copy(out=Wmk[:, k, :], in_=wstage[b][:, 0:512]),
                 reads=[("wstage", b)], writes=[("Wmk", k)])
        S.barrier()
        p0.close()
        xin = [sb1(f"xin{i}", [128, D], F32) for i in range(2)]
        junk = sb1("junk", [128, D], BF16)
        ss = [sb1(f"ss{i}", [128, 1], F32) for i in range(2)]
        rstd = [sb1(f"rstd{i}", [128, 1], F32) for i in range(2)]
        hn = [sb1(f"hn{i}", [128, D], BF16) for i in range(2)]
        hT = [sb1(f"hT{i}", [128, 8, 128], BF16) for i in range(2)]
        urow = [sb1(f"urow{i}", [128, 896], F32) for i in range(2)]
        psrow = [sb1(f"psrow{i}", [128, 2080], F32) for i in range(2)]
        for i_ in range(2):
            S.op("gpsimd", lambda e, i_=i_: e.memset(psrow[i_][:, 792:800], 0.0), writes=[("psrow_pad", i_)])
        gq128 = sb1("gq128", [128, 2, 64], F32)
        ld(gq128[:, 0, :], nsa_q_norm.partition_broadcast(128), ("gq128", 0))
        ld(gq128[:, 1, :], mem_q_norm.partition_broadcast(128), ("gq128", 1))
        S.barrier()
        psD = ps("psD")
        psE = ps("psE")
        WBF_KEYS = [("Wbf", k) for k in range(8)]
        WMK_KEYS = [("Wmk", k) for k in range(8)]

        def norm_transpose(src_ap, npart, b, gt, gkey):
            S.dma("sync", lambda e: e.dma_start(out=xin[b][0:npart, :], in_=src_ap),
                  ("xin", b), writes=[("xin", b)])
            S.op("scalar", lambda e: e.activation(out=junk[0:npart, :], in_=xin[b][0:npart, :],
                                                  func=AF.Square, accum_out=ss[b][0:npart, :]),
                 reads=[("xin", b)], writes=["junk", ("ss", b)])
            S.op("scalar", lambda e: e.activation(out=ss[b][0:npart, :], in_=ss[b][0:npart, :],
                                                  func=AF.Sqrt, bias=epsT[0:npart, :], scale=1.0 / D),
                 reads=[("ss", b), "epsT"], writes=[("ss", b)])
            S.op("vector", lambda e: e.reciprocal(out=rstd[b][0:npart, :], in_=ss[b][0:npart, :]),
                 reads=[("ss", b)], writes=[("rstd", b)])
            S.op("vector", lambda e: e.tensor_scalar(out=hn[b][0:npart, :], in0=xin[b][0:npart, :],
                                                     scalar1=rstd[b][0:npart, :], scalar2=None,
                                                     op0=ALU.mult),
                 reads=[("xin", b), ("rstd", b)], writes=[("hn", b)])
            psTb = psT[:].bitcast(BF16)
            for k in range(8):
                S.op("tensor", lambda e, k=k: e.transpose(out=psTb[:, k * 128:k * 128 + npart],
                                                          in_=hn[b][0:npart, k * 128:(k + 1) * 128],
                                                          identity=identB[0:npart, 0:npart]),
                     reads=[("hn", b), "identB"], writes=["psT"])
            pv = psTb.rearrange("p (k t) -> p k t", k=8)[:, :, 0:npart]
            S.op("vector", lambda e: e.tensor_tensor(out=hT[b][:, :, 0:npart], in0=pv,
                                                     in1=gt[:].unsqueeze(2).to_broadcast([128, 8, npart]),
                                                     op=ALU.mult),
                 reads=[gkey], writes=["psT", ("hT", b)])

        def proj(pst, pkey, b, npart, W, wkeys, c0, n):
            for k in range(8):
                S.op("tensor", lambda e, k=k: e.matmul(pst[0:npart, 0:n], lhsT=hT[b][:, k, 0:npart],
                                                       rhs=W[:, k, c0:c0 + n],
                                                       start=(k == 0), stop=(k == 7)),
                     reads=[("hT", b)] + wkeys, writes=[pkey])

        def knorm(b, npart, nj, gain_ap, gkey):
            S.op("gpsimd", lambda e: e.tensor_tensor(out=ksq[0:npart, 0:nj, :], in0=kraw[b][0:npart, 0:nj, :],
                                                     in1=kraw[b][0:npart, 0:nj, :], op=ALU.mult),
                 reads=[("kraw", b)], writes=["ksq"])
            S.op("vector", lambda e: e.tensor_reduce(out=kss[0:npart, 0:nj], in_=ksq[0:npart, 0:nj, :],
                                                     axis=AX.X, op=ALU.add),
                 reads=["ksq"], writes=["kss"])
            S.op("scalar", lambda e: e.activation(out=kss[0:npart, 0:nj], in_=kss[0:npart, 0:nj],
                                                  func=AF.Sqrt, bias=epsT[0:npart, :], scale=1.0 / 64),
                 reads=["kss", "epsT"], writes=["kss"])
            S.op("vector", lambda e: e.reciprocal(out=krs[0:npart, 0:nj], in_=kss[0:npart, 0:nj]),
                 reads=["kss"], writes=["krs"])
            S.op("vector", lambda e: e.tensor_tensor(
                out=kn[0:npart, 0:nj, :], in0=kraw[b][0:npart, 0:nj, :],
                in1=krs[0:npart, 0:nj].unsqueeze(2).to_broadcast([npart, nj, 64]), op=ALU.mult),
                 reads=[("kraw", b), "krs"], writes=["kn"])
            S.op("gpsimd", lambda e: e.tensor_tensor(out=kn[0:npart, 0:nj, :], in0=kn[0:npart, 0:nj, :],
                                                     in1=gain_ap, op=ALU.mult),
                 reads=["kn"] + gkey, writes=["kn"])

        def krope(npart, nj, cos_ap, sin_ap, ckeys):
            cb = cos_ap.unsqueeze(1).to_broadcast([npart, nj, 32])
            sbb = sin_ap.unsqueeze(1).to_broadcast([npart, nj, 32])
            x1 = kn[0:npart, 0:nj, 0:32]
            x2 = kn[0:npart, 0:nj, 32:64]
            o1 = kro[0:npart, 0:nj, 0:32]
            o2 = kro[0:npart, 0:nj, 32:64]
            tm = ktmp[0:npart, 0:nj, :]
            S.op("vector", lambda e: e.tensor_tensor(out=o1, in0=x1, in1=cb, op=ALU.mult),
                 reads=["kn"] + ckeys, writes=["kro1"])
            S.op("gpsimd", lambda e: e.tensor_tensor(out=tm, in0=x2, in1=sbb, op=ALU.mult),
                 reads=["kn"] + ckeys, writes=["ktmp"])
            S.op("vector", lambda e: e.tensor_tensor(out=o1, in0=o1, in1=tm, op=ALU.subtract),
                 reads=["kro1", "ktmp"], writes=["kro1"])
            S.op("gpsimd", lambda e: e.tensor_tensor(out=o2, in0=x2, in1=cb, op=ALU.mult),
                 reads=["kn"] + ckeys, writes=["kro2"])
            S.op("vector", lambda e: e.tensor_tensor(out=tm, in0=x1, in1=sbb, op=ALU.mult),
                 reads=["kn", "kro1"] + ckeys, writes=["ktmp"])
            S.op("vector", lambda e: e.tensor_tensor(out=o2, in0=o2, in1=tm, op=ALU.add),
                 reads=["kro2", "ktmp"], writes=["kro2"])

        for t in range(2 if stage >= 1 else 0):
            b = t % 2
            norm_transpose(memp[t * 128:(t + 1) * 128, :], 128, b, gmT, "gmT")
            if SUB < 2:
                continue
            proj(psA[b], ("psA", b), b, 128, Wmk, WMK_KEYS, 0, 512)
            if SUB < 3:
                continue
            S.op("scalar", lambda e, b=b: e.copy(out=rows4[b][:, :], in_=psA[b][:, :]),
                 writes=[("psA", b), ("rows4v", b), ("rows4k", b), ("rows4s", b)])
            S.op("vector", lambda e, b=b: e.tensor_copy(
                out=kraw[b][:, 0:4, :], in_=rows4[b][:, 0:256].rearrange("p (j d) -> p j d", j=4)),
                 reads=[("rows4k", b)], writes=[("kraw", b)])
            knorm(b, 128, 4, gainM[:, 0:4, :], [("gainM", j) for j in range(4)])
            S.op("gpsimd", lambda e, b=b: e.tensor_copy(
                out=rows4[b][:, 0:256].rearrange("p (j d) -> p j d", j=4), in_=kn[:, 0:4, :]),
                 reads=["kn"], writes=[("rows4k", b)])
            S.dma(OUTQ, lambda e, b=b, t=t: e.dma_start(out=o_memkv[t * 128:(t + 1) * 128, :], in_=rows4[b][:]),
                  ("st_rows4", b), reads=[("rows4v", b), ("rows4k", b), ("rows4s", b)],
                  writes=[("o_memkv", t)], is_out=True)

        RW = None
        if stage >= 2 and RWKV_ON:
            RW = {}
            V1_, G1_, A1_ = "vector", "gpsimd", "scalar"

            def tt1(eng, out, in0, in1, op, reads, writes):
                S.op(eng, lambda e: e.tensor_tensor(out=out, in0=in0, in1=in1, op=op), reads=reads, writes=writes)

            def ts1(eng, out, in0, s1, op0, reads, writes):
                S.op(eng, lambda e: e.tensor_scalar(out=out, in0=in0, scalar1=s1, scalar2=None, op0=op0),
                     reads=reads, writes=writes)

            def act1(out, in_, func, reads, writes, scale=1.0):
                S.op(A1_, lambda e: e.activation(out=out, in_=in_, func=func, scale=scale), reads=reads, writes=writes)

            def bl(name, src, n):
                t = sb1(name, [128, n], F32)
                ld(t[:], src.partition_broadcast(128), name)
                return t

            pmu = bl("pmu", rwkv_mu, 896)
            pw0 = bl("pw0", rwkv_w0, 256)
            pa0 = bl("pa0", rwkv_a0, 256)
            pkk = bl("pkk", rwkv_k_k, 256)
            pka = bl("pka", rwkv_k_a, 256)
            pw2 = sb1("pw2", [64, 2, 256], F32)
            ld(pw2[:, 0, :], rwkv_w2, ("pw2", 0))
            ld(pw2[:, 1, :], rwkv_a2, ("pw2", 1))
            shiftM = sb1("shiftM", [128, 2, 128], F32)
            ld(shiftM[:], c_shift, "shiftM")
            pu = sb1("pu", [128, 896], F32)
            plw = sb1("plw", [128, 128], F32)
            plwT = sb1("plwT", [64, 2, 128], F32)
            pxw = sb1("pxw", [128, 2, 256], F32)
            V4f = sb1("V4f", [128, 4, 256], F32)
            V4b = sb1("V4b", [128, 4, 256], BF16)
            YTp = sb1("YT_p", [128, 2, 128], F32)
            ytk = sb1("ytk", [128, 4, 64], F32)
            ysq_ = sb1("ysq_", [128, 4, 64], F32)
            st4_ = sb1("st4_", [128, 4], F32)
            gne_ = sb1("gne_", [128, 1], F32)
            S.op("vector", lambda e: e.memset(gne_[:], 64e-5), writes=["gne_"])
            plnw = bl("plnw", rwkv_ln_w, 256)
            plnb = bl("plnb", rwkv_ln_b, 256)
            prk = bl("prk", rwkv_r_k, 256)
            Wd = sb1("Wd", [128, 256], F32)
            pkt = sb1("pkt", [128, 4, 64], F32)
            pks = sb1("pks", [128, 4], F32)
            ptm = sb1("ptm", [128, 256], F32)
            vTt = sb1("vTt", [128, 2, 128], F32)
            Sst = sb1("Sst", [128, 2, 64], F32)
            prd = sb1("prd", [128, 2, 64], F32)
            sa_ = sb1("sa_", [128, 2], F32)
            S.op(V1_, lambda e: e.memset(Sst[:], 0.0), writes=["Sst"])
            S.barrier()
            RW["S"] = Sst
            slots = [(psD, "psD", psE, "psE"), (psA[1], ("psA", 1), psB[1], ("psB", 1))]

            def rwkv_tile(tt):
                ub = urow[tt % 2]
                upv = urow[(tt + 1) % 2]
                UK = [("urow", tt % 2, C_U), ("urow", tt % 2, C_U + 512)]
                UPK = [("urow", (tt + 1) % 2, C_U), ("urow", (tt + 1) % 2, C_U + 512)]
                for (c0, n, pst, pk) in ((0, 512, psC, "psC"), (512, 384, psT, "psT")):
                    S.op("tensor", lambda e, c0=c0, n=n, pst=pst: e.matmul(pst[:, 0:n], lhsT=shiftM[:, 0, :],
                                                                           rhs=ub[:, c0:c0 + n], start=True, stop=(tt == 0)),
                         reads=UK + ["shiftM"], writes=[pk])
                    if tt > 0:
                        S.op("tensor", lambda e, c0=c0, n=n, pst=pst: e.matmul(pst[:, 0:n], lhsT=shiftM[:, 1, :],
                                                                               rhs=upv[:, c0:c0 + n], start=False, stop=True),
                             reads=UPK + ["shiftM"], writes=[pk])
                    tt1(V1_, pu[:, c0:c0 + n], pst[:, 0:n], ub[:, c0:c0 + n], ALU.subtract, UK, [pk, ("pu", c0)])
                PUK = [("pu", 0), ("pu", 512)]
                tt1(G1_, pu[:], pu[:], pmu[:], ALU.mult, PUK + ["pmu"], PUK)
                tt1(G1_, pu[:], pu[:], ub[:], ALU.add, PUK + UK, PUK)
                k_ = pu[:, 256:512]
                act1(plw[:, 0:64], pu[:, 768:832], AF.Tanh, PUK, ["plw0"])
                S.op(G1_, lambda e: e.tensor_copy(out=plw[:, 64:128], in_=pu[:, 832:896]), reads=PUK, writes=["plw1"])
                for i in range(2):
                    S.op("tensor", lambda e, i=i: e.transpose(out=psC[0:64, i * 128:(i + 1) * 128],
                                                              in_=plw[:, i * 64:(i + 1) * 64], identity=identF[:]),
                         reads=["plw0", "plw1", "identF"], writes=["psC"])
                S.op(V1_, lambda e: e.tensor_copy(out=plwT[:].rearrange("p a b -> p (a b)"), in_=psC[0:64, 0:256]),
                     writes=["psC", "plwT"])
                for i in range(2):
                    S.op("tensor", lambda e, i=i: e.matmul(psT[:, i * 256:(i + 1) * 256], lhsT=plwT[:, i, :],
                                                           rhs=pw2[:, i, :], start=True, stop=True),
                         reads=["plwT", ("pw2", i)], writes=["psT"])
                tt1(V1_, pxw[:, 0, :], psT[:, 0:256], pw0[:], ALU.add, ["pw0"], ["psT", "pxw0"])
                tt1(V1_, pxw[:, 1, :], psT[:, 256:512], pa0[:], ALU.add, ["pa0"], ["psT", "pxw1"])
                act1(pxw[:].rearrange("p a b -> p (a b)"), pxw[:].rearrange("p a b -> p (a b)"), AF.Sigmoid,
                     ["pxw0", "pxw1"], ["pxw0", "pxw1"])
                act1(Wd[:], pxw[:, 0, :], AF.Exp, ["pxw0"], ["Wd"], scale=-math.exp(-0.5))
                a_g = pxw[:, 1, :]
                pkf = pkt[:].rearrange("p h d -> p (h d)")
                tt1(G1_, pkf, k_, pkk[:], ALU.mult, PUK + ["pkk"], ["pkt"])
                tt1(G1_, ptm[:], pkf, pkf, ALU.mult, ["pkt"], ["ptm"])
                S.op(V1_, lambda e: e.tensor_reduce(out=pks[:], in_=ptm[:].rearrange("p (h d) -> p h d", h=4),
                                                    axis=AX.X, op=ALU.add), reads=["ptm"], writes=["pks"])
                act1(pks[:], pks[:], AF.Sqrt, ["pks"], ["pks"])
                ts1(V1_, pks[:], pks[:], 1e-12, ALU.max, ["pks"], ["pks"])
                S.op(V1_, lambda e: e.reciprocal(out=pks[:], in_=pks[:]), reads=["pks"], writes=["pks"])
                tt1(V1_, pkt[:], pkt[:], pks[:].unsqueeze(2).to_broadcast([128, 4, 64]), ALU.mult, ["pkt", "pks"], ["pkt"])
                ts1(G1_, V4f[:, 0, :], pkf, -1.0, ALU.mult, ["pkt"], [("V4f", 0)])
                tt1(G1_, V4f[:, 1, :], pkf, a_g, ALU.mult, ["pkt", "pxw1"], [("V4f", 1)])
                ts1(V1_, ptm[:], a_g, -1.0, ALU.add, ["pxw1"], ["ptm"])
                tt1(V1_, ptm[:], ptm[:], pka[:], ALU.mult, ["ptm", "pka"], ["ptm"])
                ts1(V1_, ptm[:], ptm[:], 1.0, ALU.add, ["ptm"], ["ptm"])
                tt1(V1_, V4f[:, 2, :], ptm[:], k_, ALU.mult, ["ptm"] + PUK, [("V4f", 2)])
                S.op(G1_, lambda e: e.tensor_copy(out=V4f[:, 3, :], in_=pu[:, 0:256]), reads=PUK, writes=[("V4f", 3)])
                V4K = [("V4f", i) for i in range(4)]
                S.op(G1_, lambda e: e.tensor_copy(out=V4b[:], in_=V4f[:]), reads=V4K, writes=["V4b"])
                for hq in range(2):
                    S.op("tensor", lambda e, hq=hq: e.transpose(out=psC[:, 256 + hq * 128:256 + (hq + 1) * 128],
                                                                in_=pu[:, 512 + hq * 128:512 + (hq + 1) * 128],
                                                                identity=identF[:]),
                         reads=PUK + ["identF"], writes=["psC"])
                S.op(V1_, lambda e: e.tensor_copy(out=vTt[:].rearrange("p a b -> p (a b)"), in_=psC[:, 256:512]),
                     writes=["psC", "vTt"])
                V4v = V4b[:].rearrange("p a (hq hp k) -> p a hq hp k", hq=2, hp=2)
                Wdv = Wd[:].rearrange("p (hq hp k) -> p hq hp k", hq=2, hp=2)
                for t in range(128):
                    bA, kA, bB, kB = slots[t % 2]
                    for hp in range(2):
                        S.op("tensor", lambda e, t=t, hp=hp, bA=bA: e.matmul(
                            bA[hp * 64:(hp + 1) * 64, 0:512], lhsT=identB[:, t:t + 1].to_broadcast([128, 64]),
                            rhs=V4v[:, :, :, hp, :], start=True, stop=True),
                             reads=["V4b", "identB"], writes=[kA])
                        S.op("tensor", lambda e, t=t, hp=hp, bB=bB: e.matmul(
                            bB[hp * 64:(hp + 1) * 64, 0:128], lhsT=identF[:, t:t + 1].to_broadcast([128, 64]),
                            rhs=Wdv[:, :, hp, :], start=True, stop=True),
                             reads=["Wd", "identF"], writes=[kB])
                    BA = bA[:, 0:512].rearrange("p (a hq k) -> p a hq k", a=4, hq=2)
                    tt1(V1_, prd[:], Sst[:], BA[:, 0, :, :], ALU.mult, ["Sst"], [kA, "prd"])
                    S.op(V1_, lambda e: e.tensor_reduce(out=sa_[:], in_=prd[:], axis=AX.X, op=ALU.add),
                         reads=["prd"], writes=["sa_"])
                    tt1(V1_, Sst[:], Sst[:], bB[:, 0:128].rearrange("p (hq k) -> p hq k", hq=2), ALU.mult,
                        ["Sst"], [kB, "Sst"])
                    for hq in range(2):
                        S.op(V1_, lambda e, hq=hq, BA=BA: e.scalar_tensor_tensor(
                            out=Sst[:, hq, :], in0=BA[:, 1, hq, :], scalar=sa_[:, hq:hq + 1], in1=Sst[:, hq, :],
                            op0=ALU.mult, op1=ALU.add), reads=["sa_", "Sst"], writes=[kA, "Sst"])
                    for hq in range(2):
                        S.op(V1_, lambda e, hq=hq, BA=BA, t=t: e.scalar_tensor_tensor(
                            out=Sst[:, hq, :], in0=BA[:, 2, hq, :], scalar=vTt[:, hq, t:t + 1], in1=Sst[:, hq, :],
                            op0=ALU.mult, op1=ALU.add), reads=["vTt", "Sst"], writes=[kA, "Sst"])
                    if YP_ON:
                        tt1(V1_, prd[:], Sst[:], BA[:, 3, :, :], ALU.mult, ["Sst", "prd"], [kA, "prd"])
                        S.op(V1_, lambda e, t=t: e.tensor_reduce(out=YTp[:, :, t], in_=prd[:], axis=AX.X, op=ALU.add),
                             reads=["prd"], writes=["YT_p"])
                if YP_ON:
                    for hq in range(2):
                        S.op("tensor", lambda e, hq=hq: e.transpose(out=psC[:, hq * 128:(hq + 1) * 128], in_=YTp[:, hq, :],
                                                                    identity=identF[:]),
                             reads=["YT_p", "identF"], writes=["psC"])
                    S.op(V1_, lambda e: e.tensor_copy(out=ytk[:].rearrange("p h d -> p (h d)"), in_=psC[:, 0:256]),
                         writes=["psC", "ytk"])
                    S.op(V1_, lambda e: e.tensor_reduce(out=st4_[:], in_=ytk[:], axis=AX.X, op=ALU.add),
                         reads=["ytk"], writes=["st4_"])
                    ts1(V1_, st4_[:], st4_[:], 1.0 / 64, ALU.mult, ["st4_"], ["st4_"])
                    tt1(V1_, ytk[:], ytk[:], st4_[:].unsqueeze(2).to_broadcast([128, 4, 64]), ALU.subtract,
                        ["ytk", "st4_"], ["ytk"])
                    tt1(G1_, ysq_[:], ytk[:], ytk[:], ALU.mult, ["ytk"], ["ysq_"])
                    S.op(V1_, lambda e: e.tensor_reduce(out=st4_[:], in_=ysq_[:], axis=AX.X, op=ALU.add),
                         reads=["ysq_", "ytk"], writes=["st4_"])
                    S.op(A1_, lambda e: e.activation(out=st4_[:], in_=st4_[:], func=AF.Sqrt, scale=1.0 / 64, bias=gne_[:]),
                         reads=["st4_", "gne_"], writes=["st4_"])
                    S.op(V1_, lambda e: e.reciprocal(out=st4_[:], in_=st4_[:]), reads=["st4_"], writes=["st4_"])
                    tt1(V1_, ytk[:], ytk[:], st4_[:].unsqueeze(2).to_broadcast([128, 4, 64]), ALU.mult,
                        ["ytk", "st4_"], ["ytk"])
                    yf_ = ytk[:].rearrange("p h d -> p (h d)")
                    tt1(G1_, yf_, yf_, plnw[:], ALU.mult, ["ytk", "plnw"], ["ytk"])
                    tt1(G1_, yf_, yf_, plnb[:], ALU.add, ["ytk", "plnb"], ["ytk"])
                    yq_ = ysq_[:].rearrange("p h d -> p (h d)")
                    tt1(G1_, yq_, pu[:, 0:256], V4f[:, 2, :], ALU.mult, PUK + [("V4f", 2), "ysq_"], ["ysq_"])
                    tt1(G1_, yq_, yq_, prk[:], ALU.mult, ["ysq_", "prk"], ["ysq_"])
                    S.op(V1_, lambda e: e.tensor_reduce(out=st4_[:], in_=ysq_[:], axis=AX.X, op=ALU.add),
                         reads=["ysq_", "ytk"], writes=["st4_"])
                    tt1(V1_, ysq_[:], pu[:, 512:768].rearrange("p (h d) -> p h d", h=4),
                        st4_[:].unsqueeze(2).to_broadcast([128, 4, 64]), ALU.mult, PUK + ["st4_", "ysq_"], ["ysq_"])
                    tt1(V1_, psrow[tt % 2][:, 1824:2080], yf_, yq_, ALU.add, ["ytk", "ysq_"], [("psrow_rw", tt % 2)])

        for tt in range(NT if stage >= 2 else 0):
            b = tt % 2
            norm_transpose(xp[tt * 128:(tt + 1) * 128, :], 128, b, gT, "gT")
            proj(psA[0], ("psA", 0), b, 128, Wbf, WBF_KEYS, C_KV, 512)
            proj(psB[0], ("psB", 0), b, 128, Wbf, WBF_KEYS, C_KV + 512, 256)
            S.op("scalar", lambda e, b=b: e.copy(out=rows4[b][:, :], in_=psA[0][:, :]),
                 writes=[("psA", 0), ("rows4v", b), ("rows4k", b), ("rows4s", b)])
            S.op("vector", lambda e, b=b: e.tensor_copy(out=winr[b][:, :], in_=psB[0][:, 0:256]),
                 writes=[("psB", 0), ("winrv", b), ("winrk", b)])
            S.op("gpsimd", lambda e, b=b: e.tensor_copy(
                out=kraw[b][:, 0:2, :], in_=rows4[b][:, 256:384].rearrange("p (j d) -> p j d", j=2)),
                 reads=[("rows4s", b)], writes=[("kraw", b)])
            S.op("gpsimd", lambda e, b=b: e.tensor_copy(
                out=kraw[b][:, 2:4, :], in_=winr[b][:, 0:128].rearrange("p (j d) -> p j d", j=2)),
                 reads=[("winrk", b), ("kraw", b)], writes=[("kraw", b)])
            last4 = tt >= NT - 4
            knorm(b, 128, 4, gainK[:, 0:4, :], [("gainK", j) for j in range(4)])
            krope(128, 4, cosT[:, tt, :], sinT[:, tt, :], ["cosT", "sinT"])
            S.op("gpsimd", lambda e, b=b: e.tensor_copy(
                out=rows4[b][:, 256:384].rearrange("p (j d) -> p j d", j=2), in_=kro[:, 0:2, :]),
                 reads=["kro1", "kro2"], writes=[("rows4s", b)])
            S.dma(OUTQ, lambda e, b=b, tt=tt: e.dma_start(out=o_nsa_p[tt * 128:(tt + 1) * 128, :], in_=rows4[b][:]),
                  ("st_rows4", b), reads=[("rows4v", b), ("rows4k", b), ("rows4s", b)],
                  writes=[("o_nsa_p", tt)], is_out=True)
            if last4 or YP_ON:
                S.op("gpsimd", lambda e, b=b: e.tensor_copy(
                    out=winr[b][:, 0:128].rearrange("p (j d) -> p j d", j=2), in_=kro[:, 2:4, :]),
                     reads=["kro1", "kro2"], writes=[("winrk", b)])
            if YP_ON:
                S.dma(OUTQ, lambda e, b=b, tt=tt: e.dma_start(out=scr_win[tt * 128:(tt + 1) * 128, :], in_=winr[b][:]),
                      ("st_winr", b), reads=[("winrv", b), ("winrk", b)], writes=[("scr_win", tt)])
            if last4:
                w0 = (tt - (NT - 4)) * 128
                S.dma(OUTQ, lambda e, b=b, w0=w0: e.dma_start(out=o_win_p[w0:w0 + 128, :], in_=winr[b][:]),
                      ("st_winr", b), reads=[("winrv", b), ("winrk", b)], writes=[("o_win_p", tt)], is_out=True)
            ub = urow[tt % 2]
            for (c0, n) in ((C_U, 512), (C_U + 512, 384)):
                proj(psC, "psC", b, 128, Wbf, WBF_KEYS, c0, n)
                S.op("scalar", lambda e, c0=c0, n=n, ub=ub: e.copy(out=ub[:, c0 - C_U:c0 - C_U + n], in_=psC[:, 0:n]),
                     writes=["psC", ("urow", tt % 2, c0)])
            if tt == NT - 1:
                S.dma(OUTQ, lambda e, ub=ub: e.dma_start(out=o_shift_p[:, :], in_=ub[127:128, :]),
                      "st_urow", reads=[("urow", tt % 2, C_U), ("urow", tt % 2, C_U + 512)],
                      writes=["o_shift_p"], is_out=True)
            if YP_ON:
                pr = psrow[tt % 2]
                PRK = [("psrow_q", tt % 2), ("psrow_g", tt % 2), ("psrow_z", tt % 2), ("psrow_m", tt % 2)]

                def actp(out, in_, func, pkey, wkey):
                    S.op("scalar", lambda e: e.activation(out=out, in_=in_, func=func), writes=[pkey, wkey])

                proj(psA[0], ("psA", 0), b, 128, Wbf, WBF_KEYS, C_Q, 512)
                S.op("vector", lambda e, b=b: e.tensor_copy(
                    out=kraw[b][:, 0:8, :], in_=psA[0][:, :].rearrange("p (j d) -> p j d", j=8)),
                     writes=[("psA", 0), ("kraw", b)])
                knorm(b, 128, 8, gq128[:, 0, :].unsqueeze(1).to_broadcast([128, 8, 64]), [("gq128", 0)])
                krope(128, 8, cosT[:, tt, :], sinT[:, tt, :], ["cosT", "sinT"])
                S.op("gpsimd", lambda e, pr=pr: e.tensor_copy(out=pr[:, 0:512].rearrange("p (j d) -> p j d", j=8),
                                                              in_=kro[:, 0:8, :]),
                     reads=["kro1", "kro2"], writes=[PRK[0]])
                proj(psB[0], ("psB", 0), b, 128, Wbf, WBF_KEYS, C_G, 512)
                actp(pr[:, 768:792], psB[0][:, 0:24], AF.Sigmoid, ("psB", 0), PRK[1])
                actp(pr[:, 800:1288], psB[0][:, 24:512], AF.Silu, ("psB", 0), PRK[2])
                proj(psC, "psC", b, 128, Wbf, WBF_KEYS, C_G + 512, 24)
                actp(pr[:, 1288:1312], psC[:, 0:24], AF.Silu, "psC", ("psrow_z2", tt % 2))
                proj(psA[0], ("psA", 0), b, 128, Wbf, WBF_KEYS, C_ZR, 512)
                actp(pr[:, 1312:1568], psA[0][:, 0:256], AF.Silu, ("psA", 0), ("psrow_z3", tt % 2))
                S.op("vector", lambda e, b=b: e.tensor_copy(
                    out=kraw[b][:, 0:4, :], in_=psA[0][:, 256:512].rearrange("p (j d) -> p j d", j=4)),
                     writes=[("psA", 0), ("kraw", b)])
                knorm(b, 128, 4, gq128[:, 1, :].unsqueeze(1).to_broadcast([128, 4, 64]), [("gq128", 1)])
                S.op("gpsimd", lambda e, pr=pr: e.tensor_copy(out=pr[:, 512:768].rearrange("p (j d) -> p j d", j=4),
                                                              in_=kn[:, 0:4, :]),
                     reads=["kn"], writes=[PRK[3]])
                proj(psB[0], ("psB", 0), b, 128, Wbf, WBF_KEYS, C_ZM, 256)
                actp(pr[:, 1568:1824], psB[0][:, 0:256], AF.Silu, ("psB", 0), ("psrow_z4", tt % 2))
            if RW is not None:
                rwkv_tile(tt)
            if YP_ON:
                S.dma(OUTQ, lambda e, pr=pr, tt=tt: e.dma_start(out=scr_ps[tt * 128:(tt + 1) * 128, :], in_=pr[:]),
                      ("st_psrow", tt % 2),
                      reads=PRK + [("psrow_z2", tt % 2), ("psrow_z3", tt % 2), ("psrow_z4", tt % 2), ("psrow_rw", tt % 2),
                             ("psrow_pad", tt % 2)],
                      writes=[("scr_ps", tt)])
        if RW is not None and stage >= 2:
            S.dma(OUTQ, lambda e: e.dma_start(out=o_wkv_p.rearrange("(hq hp) v k -> (hp v) hq k", hp=2), in_=RW["S"][:]),
                  "st_wkvp", reads=["Sst"], writes=["o_wkv_p"], is_out=True)

        b = 0
        if stage < 3:
            S.barrier()
            p1.close()
            return _finish(nc, S, stack, sb, o_yp, o_ys, o_wkv_p, o_wkv_s)
        norm_transpose(xs[:, :], DB, b, gT, "gT")
        c0 = 0
        gi = 0
        while c0 < N_IN:
            n = min(512, N_IN - c0)
            pst = psA[gi % 2]
            pkey = ("psA", gi % 2)
            proj(pst, pkey, b, DB, Wbf, WBF_KEYS, c0, n)
            S.op("scalar" if gi % 2 == 0 else "vector",
                 (lambda e, c0=c0, n=n, pst=pst: e.copy(out=dproj[:, c0:c0 + n], in_=pst[0:DB, 0:n])) if gi % 2 == 0
                 else (lambda e, c0=c0, n=n, pst=pst: e.tensor_copy(out=dproj[:, c0:c0 + n], in_=pst[0:DB, 0:n])),
                 writes=[pkey, ("dproj", gi)])
            c0 += n
            gi += 1
        DPK = [("dproj", i) for i in range(gi)]
        S.dma(OUTQ, lambda e: e.dma_start(out=o_shift_s[:, :], in_=dproj[:, C_U:C_U + 896]),
              "st_dproj", reads=DPK, writes=["o_shift_s"], is_out=True)
        S.op("vector", lambda e: e.tensor_copy(
            out=kraw[0][0:DB, 0:2, :], in_=dproj[:, C_KV + 256:C_KV + 384].rearrange("p (j d) -> p j d", j=2)),
             reads=DPK, writes=[("kraw", 0)])
        S.op("vector", lambda e: e.tensor_copy(
            out=kraw[0][0:DB, 2:4, :], in_=dproj[:, C_KV + 512:C_KV + 640].rearrange("p (j d) -> p j d", j=2)),
             reads=DPK + [("kraw", 0)], writes=[("kraw", 0)])
        knorm(0, DB, 4, gainK[0:DB, 0:4, :], [("gainK", j) for j in range(4)])
        krope(DB, 4, cosd[:, 0:32], cosd[:, 32:64], ["cosd"])
        S.op("vector", lambda e: e.tensor_copy(out=rows4[0][0:DB, 0:256], in_=dproj[:, C_KV:C_KV + 256]),
             reads=DPK, writes=[("rows4k", 0)])
        S.op("vector", lambda e: e.tensor_copy(out=rows4[0][0:DB, 384:512], in_=dproj[:, C_KV + 384:C_KV + 512]),
             reads=DPK, writes=[("rows4v", 0)])
        S.op("gpsimd", lambda e: e.tensor_copy(
            out=rows4[0][0:DB, 256:384].rearrange("p (j d) -> p j d", j=2), in_=kro[0:DB, 0:2, :]),
             reads=["kro1", "kro2"], writes=[("rows4s", 0)])
        S.dma(OUTQ, lambda e: e.dma_start(out=o_nsa_s[:, :], in_=rows4[0][0:DB, :]),
              ("st_rows4", 0), reads=[("rows4v", 0), ("rows4k", 0), ("rows4s", 0)],
              writes=["o_nsa_s"], is_out=True)
        S.op("gpsimd", lambda e: e.tensor_copy(
            out=winr[0][0:DB, 0:128].rearrange("p (j d) -> p j d", j=2), in_=kro[0:DB, 2:4, :]),
             reads=["kro1", "kro2"], writes=[("winrk", 0)])
        S.op("vector", lambda e: e.tensor_copy(out=winr[0][0:DB, 128:256], in_=dproj[:, C_KV + 640:C_KV + 768]),
             reads=DPK, writes=[("winrv", 0)])
        S.dma(OUTQ, lambda e: e.dma_start(out=o_win_s[:, 511, :], in_=winr[0][0:DB, :]),
              ("st_winr", 0), reads=[("winrv", 0), ("winrk", 0)], writes=["o_win_s_new"], is_out=True)
        for bb in range(DB):
            S.dma("sync", lambda e, bb=bb: e.dma_start(out=o_win_s[bb, 0:511, :], in_=cwin[bb, 1:512, :]),
                  ("wincp", bb % 4), writes=[("o_win_s", bb)], is_out=True)

        if stage < 4:
            S.barrier()
            p1.close()
            return _finish(nc, S, stack, sb, o_yp, o_ys, o_wkv_p, o_wkv_s)
        S.barrier()
        p1.close()
        if YP_ON:
            pa = contextlib.ExitStack()

            def sba(name, shape, dt=F32):
                return pa.enter_context(nc.sbuf_tensor(name, list(shape), dt))

            def tta(eng, out, in0, in1, op, reads, writes):
                S.op(eng, lambda e: e.tensor_tensor(out=out, in0=in0, in1=in1, op=op), reads=reads, writes=writes)

            def acta(out, in_, func, reads, writes, scale=1.0, bias=None):
                if bias is None:
                    S.op("scalar", lambda e: e.activation(out=out, in_=in_, func=func, scale=scale), reads=reads, writes=writes)
                else:
                    S.op("scalar", lambda e: e.activation(out=out, in_=in_, func=func, scale=scale, bias=bias),
                         reads=reads, writes=writes)

            SCL = 0.125
            BIGS = 1.0e4
            NK = NT * 128
            KselT = sba("KselT", [128, NK], BF16)
            VselP = sba("VselP", [128, NT, 2, 65], BF16)
            KwR = sba("KwR", [128, 8, 128], BF16)
            VwR = sba("VwR", [128, 8, 2, 65], BF16)
            ckT = sba("ckT", [128, 512], BF16)
            cvA = sba("cvA", [128, 4, 2, 65], BF16)
            KmT = sba("KmT", [128, 2, 256], BF16)
            VmA = sba("VmA", [128, 2, 4, 65], BF16)
            mk = sba("mk", [128, 19, 128], BF16)
            OVb = sba("OVb", [128, 4, 128], BF16)
            for t_, nm in ((VselP, "VselP"), (VwR, "VwR"), (cvA, "cvA"), (VmA, "VmA")):
                S.op("gpsimd", lambda e, t_=t_: e.memset(t_[:], 1.0), writes=[nm])
            S.op("gpsimd", lambda e: e.memset(ckT[:], 0.0), writes=["ckT"])
            pc = contextlib.ExitStack()

            def sbc(name, shape, dt=F32):
                return pc.enter_context(nc.sbuf_tensor(name, list(shape), dt))

            XTp = [sbc(f"XTp{i}", [128, 8192], BF16) for i in range(2)]
            for i in range(2):
                S.op("gpsimd", lambda e, i=i: e.memset(XTp[i][:], 0.0), writes=[("XTp", i)])
            W1p = sbc("W1p", [128, 2, 32, 128], BF16)
            W2p = sbc("W2p", [128, 2, 128], F32)
            peTp = sbc("peTp", [128, 2, 32], BF16)
            b1Tp = sbc("b1Tp", [128, 2], F32)
            bias1p = sbc("bias1p", [128, 2], F32)
            b2Bp = sbc("b2Bp", [128, 2, 2, 64], F32)
            gckp = sbc("gckp", [128, 64], F32)
            coscp = sbc("coscp", [128, 4, 64], F32)
            w1s = sbc("w1s", [128, 16, 128], F32)
            pes = sbc("pes", [128, 32], F32)
            stg = sbc("stg", [128, 512], F32)
            r4p = [sbc(f"r4p{i}", [128, 512], F32) for i in range(2)]
            hiSp = sbc("hiSp", [128, 512], F32)
            prep = sbc("prep", [128, 512], F32)
            hidp = [sbc(f"hidp{i}", [128, 512], F32) for i in range(2)]
            cktp = sbc("cktp", [128, 4, 2, 64], F32)
            cvtp = sbc("cvtp", [128, 4, 128], F32)
            sqp = sbc("sqp", [128, 8, 64], F32)
            cksp = sbc("cksp", [128, 8], F32)
            ckrp = sbc("ckrp", [128, 4, 2, 64], F32)
            cktmpp = sbc("cktmpp", [128, 4, 2, 32], F32)
            mkv = sbc("mkv", [128, 2, 512], F32)
            ld(gckp[:], nsa_k_norm[0].partition_broadcast(128), "gckp")
            ld(coscp[:], c_cosc, "coscp")
            for e_ in range(2):
                for g_ in range(2):
                    ld(b2Bp[:, e_, g_, :], cmp_b2[e_].partition_broadcast(128), ("b2Bp", e_, g_))
            S.op("gpsimd", lambda e: e.memset(W2p[:], 0.0), writes=["W2p"])
            S.barrier()
            for e_ in range(2):
                for g_ in range(2):
                    S.dma("sync", lambda e, e_=e_, g_=g_: e.dma_start(
                        out=W2p[g_ * 64:(g_ + 1) * 64, e_, g_ * 64:(g_ + 1) * 64], in_=cmp_w2[e_]),
                          "W2pg", writes=["W2p"])
                    S.dma("sync", lambda e, e_=e_, g_=g_: e.dma_start(
                        out=b1Tp[g_ * 64:(g_ + 1) * 64, e_:e_ + 1], in_=cmp_b1[e_].rearrange("(h o) -> h o", o=1),
                        allow_slow_non_contiguous=True), "b1Tpg", writes=["b1Tp"])
            S.barrier()
            for e_ in range(2):
                for hf in range(2):
                    S.op("gpsimd", lambda e: e.memset(w1s[:], 0.0), writes=["w1s"])
                    for g_ in range(2):
                        S.dma("sync", lambda e, e_=e_, g_=g_, hf=hf: e.dma_start(
                            out=w1s[g_ * 64:(g_ + 1) * 64, :, g_ * 64:(g_ + 1) * 64],
                            in_=cmp_w1[e_, hf * 16:(hf + 1) * 16].rearrange("p d h -> d p h")),
                              ("w1s", g_), writes=["w1s"])
                    S.op("vector", lambda e, e_=e_, hf=hf: e.tensor_copy(out=W1p[:, e_, hf * 16:(hf + 1) * 16, :], in_=w1s[:]),
                         reads=["w1s"], writes=[("W1p", e_)])
                for g_ in range(2):
                    S.dma("sync", lambda e, e_=e_, g_=g_: e.dma_start(
                        out=pes[g_ * 64:(g_ + 1) * 64, :], in_=cmp_pe[e_].rearrange("p d -> d p"),
                        allow_slow_non_contiguous=True), ("pes", g_), writes=["pes"])
                S.op("vector", lambda e, e_=e_: e.tensor_copy(out=peTp[:, e_, :], in_=pes[:]), reads=["pes"], writes=[("peTp", e_)])
            for ch in range(5):
                n_ = min(4, 19 - ch * 4)
                S.dma("sync", lambda e, ch=ch, n_=n_: e.dma_start(
                    out=stg[:, 0:n_ * 128].rearrange("p (a b) -> p a b", a=n_), in_=c_masks[:, ch * 4:ch * 4 + n_, :]),
                      "stg", writes=["stg"])
                S.op("vector", lambda e, ch=ch, n_=n_: e.tensor_copy(
                    out=mk[:, ch * 4:ch * 4 + n_, :], in_=stg[:, 0:n_ * 128].rearrange("p (a b) -> p a b", a=n_)),
                     reads=["stg"], writes=["mk"])
            S.dma("sync", lambda e: e.dma_start(out=stg[:, :].rearrange("p (a b) -> p a b", a=4), in_=c_ov), "stg", writes=["stg"])
            S.op("vector", lambda e: e.tensor_copy(out=OVb[:], in_=stg[:, :].rearrange("p (a b) -> p a b", a=4)),
                 reads=["stg"], writes=["OVb"])
            for e_ in range(2):
                for p_ in range(32):
                    S.op("tensor", lambda e, e_=e_, p_=p_: e.matmul(psD[:, e_:e_ + 1], lhsT=W1p[:, e_, p_, :],
                                                                    rhs=peTp[:, e_, p_:p_ + 1],
                                                                    start=(p_ == 0), stop=(p_ == 31)),
                         reads=[("W1p", e_), ("peTp", e_)], writes=["psD"])
            tta("vector", bias1p[:], psD[:, 0:2], b1Tp[:], ALU.add, ["b1Tp"], ["psD", "bias1p"])
            S.dma("sync", lambda e: e.dma_start(out=mkv[:], in_=o_memkv.rearrange("(mt p) c -> p mt c", p=128)),
                  "mkv", reads=[("o_memkv", 0), ("o_memkv", 1)], writes=["mkv"])
            for mt in range(2):
                for hp_ in range(2):
                    S.op("tensor", lambda e, mt=mt, hp_=hp_: e.transpose(
                        out=psC[:, (mt * 2 + hp_) * 128:(mt * 2 + hp_ + 1) * 128],
                        in_=mkv[:, mt, hp_ * 128:(hp_ + 1) * 128], identity=identF[:]),
                         reads=["mkv", "identF"], writes=["psC"])
            S.op("vector", lambda e: e.tensor_copy(
                out=KmT[:].rearrange("p a (m k) -> p m a k", m=2), in_=psC[:, :].rearrange("p (m a k) -> p m a k", m=2, a=2)),
                 writes=["psC", "KmT"])
            S.op("gpsimd", lambda e: e.tensor_copy(out=VmA[:, :, :, 0:64],
                                                   in_=mkv[:, :, 256:512].rearrange("p m (h d) -> p m h d", h=4)),
                 reads=["mkv", "VmA"], writes=["VmA"])
            for t in range(NT):
                i = t % 2
                S.dma("sync", lambda e, t=t, i=i: e.dma_start(out=r4p[i][:], in_=o_nsa_p[t * 128:(t + 1) * 128, :]),
                      ("r4p", i), reads=[("o_nsa_p", t)], writes=[("r4p", i)])
                q4 = t % 4
                for (bank, bkey, c0) in ((psA[0], ("psA", 0), 0), (psA[1], ("psA", 1), 128), (psB[0], ("psB", 0), 256)):
                    S.op("tensor", lambda e, bank=bank, c0=c0, i=i, q4=q4: e.transpose(
                        out=bank[:, q4 * 128:(q4 + 1) * 128], in_=r4p[i][:, c0:c0 + 128], identity=identF[:]),
                         reads=[("r4p", i), "identF"], writes=[bkey])
                S.op("gpsimd", lambda e, t=t, i=i: e.tensor_copy(
                    out=VselP[:, t, :, 0:64], in_=r4p[i][:, 384:512].rearrange("p (g d) -> p g d", g=2)),
                     reads=[("r4p", i), "VselP"], writes=["VselP"])
                if q4 == 3 or t == NT - 1:
                    nq = q4 + 1
                    c0 = (t - q4) * 128
                    S.op("scalar", lambda e, c0=c0, nq=nq: e.copy(out=XTp[0][:, c0:c0 + nq * 128], in_=psA[0][:, 0:nq * 128]),
                         writes=[("psA", 0), ("XTp", 0)])
                    S.op("vector", lambda e, c0=c0, nq=nq: e.tensor_copy(out=XTp[1][:, c0:c0 + nq * 128], in_=psA[1][:, 0:nq * 128]),
                         writes=[("psA", 1), ("XTp", 1)])
                    S.op("scalar", lambda e, c0=c0, nq=nq: e.copy(out=KselT[:, c0:c0 + nq * 128], in_=psB[0][:, 0:nq * 128]),
                         writes=[("psB", 0), "KselT"])
            for e_ in range(2):
                for half in range(2):
                    pst = psB[half]
                    for p_ in range(16):
                        S.op("tensor", lambda e, e_=e_, half=half, p_=p_, pst=pst: e.matmul(
                            pst[:, 0:512], lhsT=W1p[:, e_, half * 16 + p_, :],
                            rhs=XTp[e_][:].rearrange("p (c r) -> p c r", r=16)[:, :, p_],
                            start=(p_ == 0), stop=(p_ == 15)),
                             reads=[("W1p", e_), ("XTp", e_)], writes=[("psB", half)])
                acta(hiSp[:], psB[1][:, :], AF.Identity, ["bias1p"], [("psB", 1), "hiSp"], bias=bias1p[:, e_:e_ + 1])
                tta("vector", prep[:, 0:511], psB[0][:, 0:511], hiSp[:, 1:512], ALU.add, ["hiSp"], [("psB", 0), "prep"])
                S.op("gpsimd", lambda e, e_=e_: e.memset(hidp[e_][:], 0.0), writes=[("hidp", e_)])
                acta(hidp[e_][:, 0:511], prep[:, 0:511], AF.Silu, ["prep"], [("hidp", e_)])
            for cc in range(4):
                S.op("tensor", lambda e, cc=cc: e.matmul(psD[:, cc * 128:(cc + 1) * 128],
                                                         lhsT=hidp[0][:, cc * 128:(cc + 1) * 128], rhs=W2p[:, 0, :],
                                                         start=True, stop=True),
                     reads=[("hidp", 0), "W2p"], writes=["psD"])
                S.op("tensor", lambda e, cc=cc: e.matmul(psE[:, cc * 128:(cc + 1) * 128],
                                                         lhsT=hidp[1][:, cc * 128:(cc + 1) * 128], rhs=W2p[:, 1, :],
                                                         start=True, stop=True),
                     reads=[("hidp", 1), "W2p"], writes=["psE"])
            B2Kp = [("b2Bp", e_, g_) for e_ in range(2) for g_ in range(2)]
            tta("vector", cktp[:].rearrange("p c g d -> p c (g d)"), psD[:, :].rearrange("p (c x) -> p c x", c=4),
                b2Bp[:, 0, :, :].rearrange("p g d -> p (g d)").unsqueeze(1).to_broadcast([128, 4, 128]), ALU.add,
                B2Kp, ["psD", "cktp"])
            tta("vector", cvtp[:], psE[:, :].rearrange("p (c x) -> p c x", c=4),
                b2Bp[:, 1, :, :].rearrange("p g d -> p (g d)").unsqueeze(1).to_broadcast([128, 4, 128]), ALU.add,
                B2Kp, ["psE", "cvtp"])
            S.op("gpsimd", lambda e: e.tensor_copy(out=cvA[:, :, :, 0:64], in_=cvtp[:].rearrange("p c (g d) -> p c g d", g=2)),
                 reads=["cvtp", "cvA"], writes=["cvA"])
            ck8 = cktp[:].rearrange("p c g d -> p (c g) d")
            tta("gpsimd", sqp[:], ck8, ck8, ALU.mult, ["cktp"], ["sqp"])
            S.op("vector", lambda e: e.tensor_reduce(out=cksp[:], in_=sqp[:], axis=AX.X, op=ALU.add), reads=["sqp"], writes=["cksp"])
            acta(cksp[:], cksp[:], AF.Sqrt, ["cksp", "epsT"], ["cksp"], scale=1.0 / 64, bias=epsT[:])
            S.op("vector", lambda e: e.reciprocal(out=cksp[:], in_=cksp[:]), reads=["cksp"], writes=["cksp"])
            tta("vector", ck8, ck8, cksp[:].unsqueeze(2).to_broadcast([128, 8, 64]), ALU.mult, ["cktp", "cksp"], ["cktp"])
            tta("gpsimd", ck8, ck8, gckp[:].unsqueeze(1).to_broadcast([128, 8, 64]), ALU.mult, ["cktp", "gckp"], ["cktp"])
            cosb = coscp[:, :, 0:32].unsqueeze(2).to_broadcast([128, 4, 2, 32])
            sinb = coscp[:, :, 32:64].unsqueeze(2).to_broadcast([128, 4, 2, 32])
            x1 = cktp[:, :, :, 0:32]
            x2 = cktp[:, :, :, 32:64]
            tta("vector", ckrp[:, :, :, 0:32], x1, cosb, ALU.mult, ["cktp", "coscp"], [("ckrp", 0)])
            tta("gpsimd", cktmpp[:], x2, sinb, ALU.mult, ["cktp", "coscp"], ["cktmpp"])
            tta("vector", ckrp[:, :, :, 0:32], ckrp[:, :, :, 0:32], cktmpp[:], ALU.subtract, [("ckrp", 0), "cktmpp"], [("ckrp", 0)])
            tta("gpsimd", ckrp[:, :, :, 32:64], x2, cosb, ALU.mult, ["cktp", "coscp"], [("ckrp", 1)])
            tta("vector", cktmpp[:], x1, sinb, ALU.mult, ["cktp", "coscp", ("ckrp", 0)], ["cktmpp"])
            tta("vector", ckrp[:, :, :, 32:64], ckrp[:, :, :, 32:64], cktmpp[:], ALU.add, [("ckrp", 1), "cktmpp"], [("ckrp", 1)])
            for cc in range(4):
                S.op("tensor", lambda e, cc=cc: e.transpose(out=psC[:, cc * 128:(cc + 1) * 128],
                                                            in_=ckrp[:, cc, :, :].rearrange("p g d -> p (g d)"), identity=identF[:]),
                     reads=[("ckrp", 0), ("ckrp", 1), "identF"], writes=["psC"])
            S.op("vector", lambda e: e.tensor_copy(out=ckT[:], in_=psC[:, :]), reads=["ckT"], writes=["psC", "ckT"])
            S.barrier()
            pc.close()
            pl = contextlib.ExitStack()

            def sbl(name, shape, dt=F32):
                return pl.enter_context(nc.sbuf_tensor(name, list(shape), dt))

            Wob2 = sbl("Wob2", [128, 8, D], BF16)
            wos2 = [sbl(f"wos2_{i}", [128, D], F32) for i in range(2)]
            for k in range(8):
                i = k % 2
                ld(wos2[i][:], w_out[k * 128:(k + 1) * 128, :], ("wos2", i), shared=False)
                S.op("gpsimd" if k % 2 == 0 else "vector", lambda e, k=k, i=i: e.tensor_copy(out=Wob2[:, k, :], in_=wos2[i][:]),
                     reads=[("wos2", i)], writes=[("Wob2", k)])
            WOK = [("Wob2", k) for k in range(8)]
            prow = sbl("prow", [128, 2080], F32)
            xt = sbl("xt", [128, D], F32)
            wrp = sbl("wrp", [128, 256], F32)
            QT = sbl("QT", [128, 4, 128], BF16)
            qre = sbl("qre", [128, 4, 2, 64], F32)
            QmT = sbl("QmT", [128, 2, 128], BF16)
            PTc = sbl("PTc", [128, 4, 512], BF16)
            PTs = [sbl(f"PTs{i}", [128, 512], BF16) for i in range(2)]
            bmQ = [sbl(f"bmQ{i}", [128, 2, 64], F32) for i in range(2)]
            scoreP = sbl("score_p", [128, 128], F32)
            scoreP2 = sbl("score2_p", [128, 128], F32)
            bm = sbl("bm", [128, 128], F32)
            m16 = sbl("m16", [128, 16], F32)
            thr = sbl("thr", [128, 1], F32)
            rden = sbl("rden", [128, 4], F32)
            oC = sbl("oC", [128, 8, 64], F32)
            oS = sbl("oS", [128, 8, 64], F32)
            oW = sbl("oW", [128, 8, 64], F32)
            oM = sbl("oM", [128, 4, 64], F32)
            catp = sbl("catp", [128, D], F32)
            catTb = sbl("catTb", [128, 8, 128], BF16)
            yout = sbl("yout", [128, D], F32)
            QTf = QT[:].rearrange("p j q -> p (j q)")

            def evac(Obank, okey, dst, rd_keep=False):
                Ov = Obank[:, 0:260].rearrange("p (h c) -> p h c", h=4)
                S.op("vector", lambda e: e.tensor_scalar(out=rden[:], in0=Ov[:, :, 64], scalar1=1e-30, scalar2=None,
                                                         op0=ALU.max), writes=[okey, "rden"])
                S.op("vector", lambda e: e.reciprocal(out=rden[:], in_=rden[:]), reads=["rden"], writes=["rden"])
                tta("vector", dst, Ov[:, :, 0:64], rden[:].unsqueeze(2).to_broadcast([128, 4, 64]), ALU.mult,
                    ["rden"], [okey, ("obr", id(dst))])

            def attend(chunks, Obank, okey, qkeys):
                n = len(chunks)
                for ci, (KTap, Qap, VAap, masks, keep, kkeys, *pre) in enumerate(chunks):
                    if pre:
                        pre[0]()
                    stb = psA[ci % 2]
                    skey = ("psA", ci % 2)
                    S.op("tensor", lambda e, stb=stb, KTap=KTap, Qap=Qap: e.matmul(stb[:, 0:512], lhsT=KTap, rhs=Qap,
                                                                                   start=True, stop=True),
                         reads=kkeys + qkeys, writes=[skey])
                    if keep is None:
                        PT = PTs[ci % 2][:]
                        pkey = ("PTs", ci % 2)
                    else:
                        PT, pkey = keep
                    acta(PT, stb[:, 0:512], AF.Exp, [], [skey, pkey], scale=SCL)
                    PT3 = PT.rearrange("p (h q) -> p h q", h=4)
                    for (mkind, mval) in masks:
                        if mkind == "static":
                            tta("vector", PT3, PT3, mk[:, mval, :].unsqueeze(1).to_broadcast([128, 4, 128]), ALU.mult,
                                [pkey, "mk"], [pkey])
                        else:
                            tta("vector", PT3, PT3, mval.unsqueeze(1).to_broadcast([128, 4, 128]), ALU.mult,
                                [pkey], [pkey, ("psB", 1)])
                    for h in range(4):
                        S.op("tensor", lambda e, h=h, PT=PT, VAap=VAap, ci=ci: e.matmul(
                            Obank[:, h * 65:(h + 1) * 65], lhsT=PT[:, h * 128:(h + 1) * 128], rhs=VAap,
                            start=(ci == 0 and h == 0), stop=(ci == n - 1 and h == 3), skip_group_check=True),
                             reads=[pkey] + kkeys, writes=[okey])

            for t in range(NT):
                S.dma("sync", lambda e, t=t: e.dma_start(out=prow[:], in_=scr_ps[t * 128:(t + 1) * 128, :]),
                      "prow", reads=[("scr_ps", t)], writes=["prow"])
                S.dma("sync", lambda e, t=t: e.dma_start(out=xt[:], in_=xp[t * 128:(t + 1) * 128, :]), "xt", writes=["xt"])
                S.dma("sync", lambda e, t=t: e.dma_start(out=wrp[:], in_=scr_win[t * 128:(t + 1) * 128, :]),
                      "wrp", reads=[("scr_win", t)], writes=["wrp"])
                sl = t % 8
                S.op("tensor", lambda e: e.transpose(out=psC[:, 0:128], in_=wrp[:, 0:128], identity=identF[:]),
                     reads=["wrp", "identF"], writes=["psC"])
                for a in range(2):
                    S.op("tensor", lambda e, a=a: e.transpose(out=psC[:, 128 + a * 128:256 + a * 128],
                                                              in_=prow[:, 512 + a * 128:640 + a * 128], identity=identF[:]),
                         reads=["prow", "identF"], writes=["psC"])
                S.op("vector", lambda e, sl=sl: e.tensor_copy(out=KwR[:, sl, :], in_=psC[:, 0:128]), writes=["psC", ("KwR", sl)])
                S.op("vector", lambda e: e.tensor_copy(out=QmT[:].rearrange("p a q -> p (a q)"), in_=psC[:, 128:384]),
                     writes=["psC", "QmT"])
                S.op("gpsimd", lambda e, sl=sl: e.tensor_copy(out=VwR[:, sl, :, 0:64],
                                                              in_=wrp[:, 128:256].rearrange("p (g d) -> p g d", g=2)),
                     reads=["wrp", "VwR"], writes=[("VwRs", sl)])
                qv = prow[:, 0:512].rearrange("p (g j d) -> p j g d", g=2, j=4)
                S.op("gpsimd", lambda e: e.tensor_copy(out=qre[:], in_=qv), reads=["prow"], writes=["qre"])
                for j in range(4):
                    S.op("tensor", lambda e, j=j: e.transpose(out=psT[:, j * 128:(j + 1) * 128],
                                                              in_=qre[:, j, :, :].rearrange("p g d -> p (g d)"),
                                                              identity=identF[:]),
                         reads=["qre", "identF"], writes=["psT"])
                S.op("scalar", lambda e: e.copy(out=QTf, in_=psT[:, :]), writes=["psT", "QT"])
                for g in range(2):
                    go = g * 64
                    Qg = QT[go:go + 64, :, :].rearrange("p j q -> p (j q)")
                    ch = []
                    for cc in range(t // 16 + 1):
                        k_ = t - 16 * cc
                        ms = [("static", k_)] if k_ <= 16 else []
                        ch.append((ckT[go:go + 64, cc * 128:(cc + 1) * 128], Qg, cvA[:, cc, g, :], ms,
                                   (PTc[:, cc, :], ("PTc", cc)), ["ckT", "cvA"]))
                    attend(ch, psB[0], ("psB", 0), ["QT"])
                    evac(psB[0], ("psB", 0), oC[:, 4 * g:4 * g + 4, :])
                    ncc = len(ch)
                    for h in range(4):
                        for cc in range(ncc):
                            S.op("tensor", lambda e, h=h, cc=cc: e.matmul(psD[:, h * 128:(h + 1) * 128],
                                                                          lhsT=PTc[:, cc, h * 128:(h + 1) * 128],
                                                                          rhs=OVb[:, cc, :], start=(cc == 0), stop=True),
                                 reads=[("PTc", cc), "OVb"], writes=["psD"])
                    S.op("vector", lambda e: e.tensor_scalar(out=scoreP[:], in0=psD[:, 0:128], scalar1=rden[:, 0:1], scalar2=None,
                                                             op0=ALU.mult), reads=["rden"], writes=["psD", "score_p"])
                    for h in range(1, 4):
                        S.op("vector", lambda e, h=h: e.scalar_tensor_tensor(out=scoreP[:], in0=psD[:, h * 128:(h + 1) * 128],
                                                                             scalar=rden[:, h:h + 1], in1=scoreP[:],
                                                                             op0=ALU.mult, op1=ALU.add),
                             reads=["rden"], writes=["psD", "score_p"])

                    def mset(ap, val):
                        S.op("vector", lambda e: e.memset(ap, val), reads=[], writes=["score_p"])

                    mset(scoreP[:, 0:1], BIGS)
                    for half in range(2):
                        cur = 2 * t + half
                        pr_ = slice(64 * half, 64 * half + 64)
                        mset(scoreP[pr_, cur:cur + 1], BIGS)
                        if cur - 1 >= 0:
                            mset(scoreP[pr_, cur - 1:cur], BIGS)
                        if cur + 1 < 128:
                            mset(scoreP[pr_, cur + 1:128], -BIGS)
                    S.op("vector", lambda e: e.max(out=m16[:, 0:8], in_=scoreP[:]), reads=["score_p"], writes=["m16"])
                    S.op("vector", lambda e: e.match_replace(out=scoreP2[:], in_to_replace=m16[:, 0:8], in_values=scoreP[:],
                                                             imm_value=-1.0e30), reads=["score_p", "m16"], writes=["scoreP2_p"])
                    S.op("vector", lambda e: e.max(out=m16[:, 8:16], in_=scoreP2[:]), reads=["scoreP2_p", "m16"], writes=["m16"])
                    S.op("vector", lambda e: e.tensor_scalar(out=thr[:], in0=m16[:, 15:16], scalar1=-BIGS / 2, scalar2=None,
                                                             op0=ALU.max), reads=["m16"], writes=["thr"])
                    S.op("vector", lambda e: e.tensor_scalar(out=bm[:], in0=scoreP[:], scalar1=thr[:, 0:1], scalar2=None,
                                                             op0=ALU.is_ge), reads=["score_p", "thr"], writes=["bm"])
                    ch = []
                    for c in range(t + 1):
                        i2 = c % 2

                        def mkmask(c=c, i2=i2):
                            S.op("vector", lambda e: e.tensor_copy(
                                out=bmQ[i2][:], in_=bm[:, 2 * c:2 * c + 2].unsqueeze(2).to_broadcast([128, 2, 64])),
                                 reads=["bm"], writes=[("bmQ", i2)])
                            S.op("tensor", lambda e: e.transpose(out=psB[1][:, i2 * 128:(i2 + 1) * 128],
                                                                 in_=bmQ[i2][:].rearrange("p a b -> p (a b)"),
                                                                 identity=identF[:]),
                                 reads=[("bmQ", i2), "identF"], writes=[("psB", 1)])

                        ms = [("psum", psB[1][:, i2 * 128:(i2 + 1) * 128])]
                        if c == t:
                            ms.append(("static", 17))
                        ch.append((KselT[go:go + 64, c * 128:(c + 1) * 128], Qg, VselP[:, c, g, :], ms, None,
                                   ["KselT", "VselP"], mkmask))
                    attend_sel = ch
                    attend(attend_sel, psB[0], ("psB", 0), ["QT"])
                    evac(psB[0], ("psB", 0), oS[:, 4 * g:4 * g + 4, :])
                    ch = []
                    for c in range(max(0, t - 4), t + 1):
                        ms = []
                        if c == t:
                            ms.append(("static", 17))
                        if c == t - 4:
                            ms.append(("static", 18))
                        s8 = c % 8
                        ch.append((KwR[go:go + 64, s8, :], Qg, VwR[:, s8, g, :], ms, None, [("KwR", s8), ("VwRs", s8), "VwR"]))
                    attend(ch, psB[0], ("psB", 0), ["QT"])
                    evac(psB[0], ("psB", 0), oW[:, 4 * g:4 * g + 4, :])
                for mt in range(2):
                    for h in range(4):
                        po = (h % 2) * 64
                        stb = psA[h % 2]
                        S.op("tensor", lambda e, h=h, po=po, mt=mt, stb=stb: e.matmul(
                            stb[:, (h // 2) * 128:(h // 2 + 1) * 128], lhsT=KmT[po:po + 64, h // 2, mt * 128:(mt + 1) * 128],
                            rhs=QmT[po:po + 64, h // 2, :], start=True, stop=True, skip_group_check=True),
                             reads=["KmT", "QmT"], writes=[("psA", h % 2)])
                    PTv = PTs[mt][:].rearrange("p (a b q) -> p b a q", a=2, b=2)
                    for hb in range(2):
                        acta(PTv[:, hb, :, :], psA[hb][:, 0:256].rearrange("p (a q) -> p a q", a=2), AF.Exp, [],
                             [("psA", hb), ("PTs", mt)], scale=SCL)
                    for h in range(4):
                        S.op("tensor", lambda e, h=h, mt=mt: e.matmul(
                            psE[:, h * 65:(h + 1) * 65], lhsT=PTs[mt][:, h * 128:(h + 1) * 128], rhs=VmA[:, mt, h, :],
                            start=(mt == 0 and h == 0), stop=(mt == 1 and h == 3), skip_group_check=True),
                             reads=[("PTs", mt), "VmA"], writes=["psE"])
                evac(psE, "psE", oM[:, :, :])
                g3 = prow[:, 768:792].rearrange("p (h k) -> p h k", k=3)
                OK_ = [("obr", id(oC[:, 0:4, :])), ("obr", id(oC[:, 4:8, :]))]
                allobr = [k_ for k_ in S.last_w.keys() if isinstance(k_, tuple) and k_ and k_[0] == "obr"]
                tta("vector", oC[:], oC[:], g3[:, :, 0:1].to_broadcast([128, 8, 64]), ALU.mult, allobr + ["prow"], ["oCg"])
                tta("gpsimd", oS[:], oS[:], g3[:, :, 1:2].to_broadcast([128, 8, 64]), ALU.mult, allobr + ["prow"], ["oSg"])
                tta("vector", oW[:], oW[:], g3[:, :, 2:3].to_broadcast([128, 8, 64]), ALU.mult, allobr + ["prow"], ["oWg"])
                tta("vector", oC[:], oC[:], oS[:], ALU.add, ["oCg", "oSg"], ["oCg"])
                tta("vector", oC[:], oC[:], oW[:], ALU.add, ["oCg", "oWg"], ["oCg"])
                tta("vector", catp[:, 0:512], oC[:].rearrange("p h d -> p (h d)"), prow[:, 800:1312], ALU.mult,
                    ["oCg", "prow"], [("catp", 0)])
                tta("gpsimd", catp[:, 512:768], prow[:, 1824:2080], prow[:, 1312:1568], ALU.mult, ["prow"], [("catp", 1)])
                tta("gpsimd", catp[:, 768:1024], oM[:].rearrange("p h d -> p (h d)"), prow[:, 1568:1824], ALU.mult,
                    allobr + ["prow"], [("catp", 2)])
                CK = [("catp", 0), ("catp", 1), ("catp", 2)]
                for rnd in range(2):
                    for k4 in range(4):
                        k = rnd * 4 + k4
                        S.op("tensor", lambda e, k=k, k4=k4: e.transpose(out=psT[:, k4 * 128:(k4 + 1) * 128],
                                                                         in_=catp[:, k * 128:(k + 1) * 128], identity=identF[:]),
                             reads=CK + ["identF"], writes=["psT"])
                    S.op("scalar", lambda e, rnd=rnd: e.copy(out=catTb[:, rnd * 4:rnd * 4 + 4, :].rearrange("p k q -> p (k q)"),
                                                             in_=psT[:, :]), writes=["psT", ("catTb", rnd)])
                for half in range(2):
                    for k in range(8):
                        S.op("tensor", lambda e, k=k, half=half: e.matmul(psE[:, 0:512], lhsT=catTb[:, k, :],
                                                                          rhs=Wob2[:, k, half * 512:(half + 1) * 512],
                                                                          start=(k == 0), stop=(k == 7)),
                             reads=[("catTb", 0), ("catTb", 1)] + WOK, writes=["psE"])
                    tta("vector", yout[:, half * 512:(half + 1) * 512], psE[:, 0:512], xt[:, half * 512:(half + 1) * 512],
                        ALU.add, ["xt"], ["psE", ("yout", half)])
                S.dma(OUTQ, lambda e, t=t: e.dma_start(out=o_yp[t * 128:(t + 1) * 128, :], in_=yout[:]),
                      "st_yout", reads=[("yout", 0), ("yout", 1)], writes=[("o_yp", t)], is_out=True)
            S.barrier()
            pl.close()
            pa.close()
        cat = sb("cat", [16, D], F32)
        sz = sb("sz", [16, D], F32)
        gts = sb("gts", [16, 24], F32)
        QM = sb("QM", [16, 768], F32)
        newkv = sb("newkv", [16, 512], F32)
        onehot = sb("onehot", [16, 16, 128], F32)
        p2 = contextlib.ExitStack()

        def sb2(name, shape, dt=F32):
            return p2.enter_context(nc.sbuf_tensor(name, list(shape), dt))

        V1 = "vector"
        G1 = "gpsimd"
        A1 = "scalar"

        def tt(eng, out, in0, in1, op, reads, writes):
            S.op(eng, lambda e: e.tensor_tensor(out=out, in0=in0, in1=in1, op=op), reads=reads, writes=writes)

        def ts(eng, out, in0, s1, s2, op0, op1, reads, writes):
            if op1 is None:
                S.op(eng, lambda e: e.tensor_scalar(out=out, in0=in0, scalar1=s1, scalar2=None, op0=op0),
                     reads=reads, writes=writes)
            else:
                S.op(eng, lambda e: e.tensor_scalar(out=out, in0=in0, scalar1=s1, scalar2=s2, op0=op0, op1=op1),
                     reads=reads, writes=writes)

        def act(out, in_, func, reads, writes, scale=1.0, bias=None):
            if bias is None:
                S.op(A1, lambda e: e.activation(out=out, in_=in_, func=func, scale=scale), reads=reads, writes=writes)
            else:
                S.op(A1, lambda e: e.activation(out=out, in_=in_, func=func, scale=scale, bias=bias),
                     reads=reads, writes=writes)

        def bload(name, src, n):
            t = sb2(name, [16, n], F32)
            ld(t[:], src.partition_broadcast(16), name)
            return t

        mu_b = bload("mu_b", rwkv_mu, 896)
        w0_b = bload("w0_b", rwkv_w0, 256)
        a0_b = bload("a0_b", rwkv_a0, 256)
        kk_b = bload("kk_b", rwkv_k_k, 256)
        ka_b = bload("ka_b", rwkv_k_a, 256)
        rk_b = bload("rk_b", rwkv_r_k, 256)
        lnw_b = bload("lnw_b", rwkv_ln_w, 256)
        lnb_b = bload("lnb_b", rwkv_ln_b, 256)
        w2_sb = sb2("w2_sb", [64, 256], F32)
        a2_sb = sb2("a2_sb", [64, 256], F32)
        ld(w2_sb[:], rwkv_w2, "w2_sb")
        ld(a2_sb[:], rwkv_a2, "a2_sb")
        ld(onehot[:], c_onehot, "onehot")
        prev = sb2("prev", [16, 896], F32)
        ld(prev[:], st_shift, "prev")
        gq = sb2("gq", [16, 2, 64], F32)
        ld(gq[:, 0, :], nsa_q_norm.partition_broadcast(16), ("gq", 0))
        ld(gq[:, 1, :], mem_q_norm.partition_broadcast(16), ("gq", 1))
        S.barrier()
        u = sb2("u", [16, 896], F32)
        ucur = dproj[:, C_U:C_U + 896]
        tt(V1, u[:], prev[:], ucur, ALU.subtract, DPK + ["prev"], ["u"])
        tt(V1, u[:], u[:], mu_b[:], ALU.mult, ["u", "mu_b"], ["u"])
        tt(V1, u[:], u[:], ucur, ALU.add, DPK + ["u"], ["u"])
        r_ = u[:, 0:256]
        k_ = u[:, 256:512]
        v_ = u[:, 512:768]
        lw = sb2("lw", [16, 128], F32)
        act(lw[:, 0:64], u[:, 768:832], AF.Tanh, ["u"], ["lw0"])
        S.op(V1, lambda e: e.tensor_copy(out=lw[:, 64:128], in_=u[:, 832:896]), reads=["u"], writes=["lw1"])
        lwT = sb2("lwT", [64, 2, 16], F32)
        for i in range(2):
            S.op("tensor", lambda e, i=i: e.transpose(out=psD[0:64, i * 16:(i + 1) * 16], in_=lw[:, i * 64:(i + 1) * 64],
                                                      identity=identF[0:16, 0:16]),
                 reads=["lw0", "lw1", "identF"], writes=["psD"])
        S.op(V1, lambda e: e.tensor_copy(out=lwT[:].rearrange("p a b -> p (a b)"), in_=psD[0:64, 0:32]),
             writes=["psD", "lwT"])
        S.op("tensor", lambda e: e.matmul(psE[0:16, 0:256], lhsT=lwT[:, 0, :], rhs=w2_sb[:], start=True, stop=True),
             reads=["lwT", "w2_sb"], writes=["psE"])
        S.op("tensor", lambda e: e.matmul(psE[0:16, 256:512], lhsT=lwT[:, 1, :], rhs=a2_sb[:], start=True, stop=True),
             reads=["lwT", "a2_sb"], writes=["psE"])
        V5 = sb2("V5", [16, 5, 256], F32)
        xw = sb2("xw", [16, 2, 256], F32)
        tt(V1, xw[:, 0, :], psE[0:16, 0:256], w0_b[:], ALU.add, ["w0_b"], ["psE", "xw0"])
        tt(V1, xw[:, 1, :], psE[0:16, 256:512], a0_b[:], ALU.add, ["a0_b"], ["psE", "xw1"])
        act(xw[:].rearrange("p a b -> p (a b)"), xw[:].rearrange("p a b -> p (a b)"), AF.Sigmoid,
            ["xw0", "xw1"], ["xw0", "xw1"])
        act(V5[:, 0, :], xw[:, 0, :], AF.Exp, ["xw0"], [("V5", 0)], scale=-math.exp(-0.5))
        a_g = xw[:, 1, :]
        kkt = sb2("kkt", [16, 4, 64], F32)
        kksq = sb2("kksq", [16, 4, 64], F32)
        kks = sb2("kks", [16, 4], F32)
        tt(V1, kkt[:].rearrange("p h d -> p (h d)"), k_, kk_b[:], ALU.mult, ["u", "kk_b"], ["kkt"])
        tt(V1, kksq[:], kkt[:], kkt[:], ALU.mult, ["kkt"], ["kksq"])
        S.op(V1, lambda e: e.tensor_reduce(out=kks[:], in_=kksq[:], axis=AX.X, op=ALU.add), reads=["kksq"], writes=["kks"])
        act(kks[:], kks[:], AF.Sqrt, ["kks"], ["kks"])
        ts(V1, kks[:], kks[:], 1e-12, None, ALU.max, None, ["kks"], ["kks"])
        S.op(V1, lambda e: e.reciprocal(out=kks[:], in_=kks[:]), reads=["kks"], writes=["kks"])
        tt(V1, kkt[:], kkt[:], kks[:].unsqueeze(2).to_broadcast([16, 4, 64]), ALU.mult, ["kkt", "kks"], ["kkt"])
        ts(V1, V5[:, 1, :], kkt[:].rearrange("p h d -> p (h d)"), -1.0, None, ALU.mult, None, ["kkt"], [("V5", 1)])
        tt(V1, V5[:, 2, :], kkt[:].rearrange("p h d -> p (h d)"), a_g, ALU.mult, ["kkt", "xw1"], [("V5", 2)])
        tmpk = sb2("tmpk", [16, 256], F32)
        ts(V1, tmpk[:], a_g, -1.0, None, ALU.add, None, ["xw1"], ["tmpk"])
        tt(V1, tmpk[:], tmpk[:], ka_b[:], ALU.mult, ["tmpk", "ka_b"], ["tmpk"])
        ts(V1, tmpk[:], tmpk[:], 1.0, None, ALU.add, None, ["tmpk"], ["tmpk"])
        tt(V1, V5[:, 3, :], tmpk[:], k_, ALU.mult, ["tmpk", "u"], [("V5", 3)])
        S.op(V1, lambda e: e.tensor_copy(out=V5[:, 4, :], in_=r_), reads=["u"], writes=[("V5", 4)])
        V5K = [("V5", i) for i in range(5)]
        vT = sb2("vT", [64, 4, 16], F32)
        for h in range(4):
            S.op("tensor", lambda e, h=h: e.transpose(out=psD[0:64, 64 + h * 16:64 + (h + 1) * 16],
                                                      in_=u[:, 512 + h * 64:512 + (h + 1) * 64],
                                                      identity=identF[0:16, 0:16]),
                 reads=["u", "identF"], writes=["psD"])
        S.op(V1, lambda e: e.tensor_copy(out=vT[:].rearrange("p h b -> p (h b)"), in_=psD[0:64, 64:128]),
             writes=["psD", "vT"])
        YT = sb2("YT", [64, 4, 16], F32)
        B5 = [sb2(f"B5_{i}", [64, 5, 4, 64], F32) for i in range(2)]
        Sb = [sb2(f"Sb{i}", [64, 4, 64], F32) for i in range(2)]
        S1 = [sb2(f"S1_{i}", [64, 4, 64], F32) for i in range(2)]
        prodt = sb2("prodt", [64, 4, 64], F32)
        sat = sb2("sat", [64, 4], F32)
        V5f = V5[:].rearrange("p a b -> p (a b)")
        psBC = [psA[0], psA[1], psB[0]]
        for b_ in range(DB):
            i = b_ % 2
            S.dma("sync", lambda e, b_=b_, i=i: e.dma_start(out=Sb[i][:], in_=st_wkv[b_].rearrange("h v k -> v h k")),
                  ("Sb", i), writes=[("Sb", i)])
            for j, (c0, n) in enumerate(((0, 512), (512, 512), (1024, 256))):
                S.op("tensor", lambda e, j=j, c0=c0, n=n, b_=b_: e.matmul(
                    psBC[j][0:64, 0:n], lhsT=onehot[:, b_, 0:64], rhs=V5f[:, c0:c0 + n], start=True, stop=True),
                     reads=V5K + ["onehot"], writes=[("psBC", j)])
                S.op(A1 if j != 1 else V1,
                     (lambda e, j=j, c0=c0, n=n, i=i: e.copy(
                         out=B5[i][:].rearrange("p a h k -> p (a h k)")[:, c0:c0 + n], in_=psBC[j][0:64, 0:n])) if j != 1
                     else (lambda e, j=j, c0=c0, n=n, i=i: e.tensor_copy(
                         out=B5[i][:].rearrange("p a h k -> p (a h k)")[:, c0:c0 + n], in_=psBC[j][0:64, 0:n])),
                     writes=[("psBC", j), ("B5", i, j)])
            B5K = [("B5", i, j) for j in range(3)]
            tt(V1, prodt[:], Sb[i][:], B5[i][:, 1, :, :], ALU.mult, [("Sb", i)] + B5K, ["prodt"])
            S.op(V1, lambda e: e.tensor_reduce(out=sat[:], in_=prodt[:], axis=AX.X, op=ALU.add),
                 reads=["prodt"], writes=["sat"])
            tt(G1, S1[i][:], Sb[i][:], B5[i][:, 0, :, :], ALU.mult, [("Sb", i)] + B5K, [("S1", i)])
            tt(V1, prodt[:], B5[i][:, 2, :, :], sat[:].unsqueeze(2).to_broadcast([64, 4, 64]), ALU.mult,
               B5K + ["sat", "prodt"], ["prodt"])
            tt(V1, S1[i][:], S1[i][:], prodt[:], ALU.add, [("S1", i), "prodt"], [("S1", i)])
            tt(V1, prodt[:], B5[i][:, 3, :, :], vT[:, :, b_].unsqueeze(2).to_broadcast([64, 4, 64]), ALU.mult,
               B5K + ["vT", "prodt"], ["prodt"])
            tt(V1, S1[i][:], S1[i][:], prodt[:], ALU.add, [("S1", i), "prodt"], [("S1", i)])
            S.dma(OUTQ, lambda e, b_=b_, i=i: e.dma_start(out=o_wkv_s[b_].rearrange("h v k -> v h k"), in_=S1[i][:]),
                  ("st_S1", i), reads=[("S1", i)], writes=[("o_wkv_s", b_)], is_out=True)
            tt(V1, prodt[:], S1[i][:], B5[i][:, 4, :, :], ALU.mult, [("S1", i)] + B5K + ["prodt"], ["prodt"])
            S.op(V1, lambda e, b_=b_: e.tensor_reduce(out=YT[:, :, b_], in_=prodt[:], axis=AX.X, op=ALU.add),
                 reads=["prodt"], writes=["YT"])
        ytok = sb2("ytok", [16, 4, 64], F32)
        for h in range(4):
            S.op("tensor", lambda e, h=h: e.transpose(out=psD[0:16, 128 + h * 64:128 + (h + 1) * 64], in_=YT[:, h, :],
                                                      identity=identF[0:64, 0:64]),
                 reads=["YT", "identF"], writes=["psD"])
        S.op(V1, lambda e: e.tensor_copy(out=ytok[:].rearrange("p h d -> p (h d)"), in_=psD[0:16, 128:384]),
             writes=["psD", "ytok"])
        gne = sb2("gne", [16, 1], F32)
        S.op(V1, lambda e: e.memset(gne[:], 64e-5), writes=["gne"])
        st4 = sb2("st4", [16, 4], F32)
        ysq = sb2("ysq", [16, 4, 64], F32)
        S.op(V1, lambda e: e.tensor_reduce(out=st4[:], in_=ytok[:], axis=AX.X, op=ALU.add), reads=["ytok"], writes=["st4"])
        ts(V1, st4[:], st4[:], 1.0 / 64, None, ALU.mult, None, ["st4"], ["st4"])
        tt(V1, ytok[:], ytok[:], st4[:].unsqueeze(2).to_broadcast([16, 4, 64]), ALU.subtract, ["ytok", "st4"], ["ytok"])
        tt(V1, ysq[:], ytok[:], ytok[:], ALU.mult, ["ytok"], ["ysq"])
        S.op(V1, lambda e: e.tensor_reduce(out=st4[:], in_=ysq[:], axis=AX.X, op=ALU.add), reads=["ysq", "ytok"], writes=["st4"])
        act(st4[:], st4[:], AF.Sqrt, ["st4", "gne"], ["st4"], scale=1.0 / 64, bias=gne[:])
        S.op(V1, lambda e: e.reciprocal(out=st4[:], in_=st4[:]), reads=["st4"], writes=["st4"])
        tt(V1, ytok[:], ytok[:], st4[:].unsqueeze(2).to_broadcast([16, 4, 64]), ALU.mult, ["ytok", "st4"], ["ytok"])
        yf = ytok[:].rearrange("p h d -> p (h d)")
        tt(V1, yf, yf, lnw_b[:], ALU.mult, ["ytok", "lnw_b"], ["ytok"])
        tt(V1, yf, yf, lnb_b[:], ALU.add, ["ytok", "lnb_b"], ["ytok"])
        tt(V1, ysq[:].rearrange("p h d -> p (h d)"), r_, V5[:, 3, :], ALU.mult, ["u", ("V5", 3), "ysq"], ["ysq"])
        tt(V1, ysq[:].rearrange("p h d -> p (h d)"), ysq[:].rearrange("p h d -> p (h d)"), rk_b[:], ALU.mult,
           ["ysq", "rk_b"], ["ysq"])
        S.op(V1, lambda e: e.tensor_reduce(out=st4[:], in_=ysq[:], axis=AX.X, op=ALU.add), reads=["ysq", "ytok"], writes=["st4"])
        tt(V1, ysq[:], u[:, 512:768].rearrange("p (h d) -> p h d", h=4),
           st4[:].unsqueeze(2).to_broadcast([16, 4, 64]), ALU.mult, ["u", "st4", "ysq"], ["ysq"])
        tt(V1, cat[:, 512:768], yf, ysq[:].rearrange("p h d -> p (h d)"), ALU.add, ["ytok", "ysq"], [("cat", 1)])
        act(sz[:, 0:512], dproj[:, C_ZN:C_ZN + 512], AF.Silu, DPK, [("sz", 0)])
        act(sz[:, 512:768], dproj[:, C_ZR:C_ZR + 256], AF.Silu, DPK, [("sz", 1)])
        act(sz[:, 768:1024], dproj[:, C_ZM:C_ZM + 256], AF.Silu, DPK, [("sz", 2)])
        act(gts[:], dproj[:, C_G:C_G + 24], AF.Sigmoid, DPK, ["gts"])
        S.op(V1, lambda e: e.tensor_copy(out=kraw[1][0:DB, 0:8, :],
                                         in_=dproj[:, 0:512].rearrange("p (j d) -> p j d", j=8)),
             reads=DPK, writes=[("kraw", 1)])
        knorm(1, DB, 8, gq[:, 0, :].unsqueeze(1).to_broadcast([DB, 8, 64]), [("gq", 0)])
        krope(DB, 8, cosd[:, 0:32], cosd[:, 32:64], ["cosd"])
        S.op(V1, lambda e: e.tensor_copy(out=QM[:, 0:512].rearrange("p (j d) -> p j d", j=8), in_=kro[0:DB, 0:8, :]),
             reads=["kro1", "kro2"], writes=[("QM", 0)])
        S.op(V1, lambda e: e.tensor_copy(out=kraw[1][0:DB, 0:4, :],
                                         in_=dproj[:, C_QM:C_QM + 256].rearrange("p (j d) -> p j d", j=4)),
             reads=DPK, writes=[("kraw", 1)])
        knorm(1, DB, 4, gq[:, 1, :].unsqueeze(1).to_broadcast([DB, 4, 64]), [("gq", 1)])
        S.op(V1, lambda e: e.tensor_copy(out=QM[:, 512:768].rearrange("p (j d) -> p j d", j=4), in_=kn[0:DB, 0:4, :]),
             reads=["kn"], writes=[("QM", 1)])
        S.op(V1, lambda e: e.tensor_copy(out=newkv[:, 0:256], in_=rows4[0][0:DB, 256:512]),
             reads=[("rows4s", 0), ("rows4v", 0)], writes=[("newkv", 0)])
        S.op(V1, lambda e: e.tensor_copy(out=newkv[:, 256:512], in_=winr[0][0:DB, :]),
             reads=[("winrk", 0), ("winrv", 0)], writes=[("newkv", 1)])
        S.barrier()
        p2.close()
        if stage < 5:
            return _finish(nc, S, stack, sb, o_yp, o_ys, o_wkv_p, None)

        p3 = contextlib.ExitStack()

        def sb3(name, shape, dt=F32):
            return p3.enter_context(nc.sbuf_tensor(name, list(shape), dt))

        SCALE = 64 ** -0.5
        BIG = 1.0e4
        cst = sb3("cst", [128, 8], F32)
        ld(cst[:], c_valid, "cst")
        OV = sb3("OV", [128, 4, 128], F32)
        ld(OV[:], c_ov, "OV")
        cosc = sb3("cosc", [128, 4, 64], F32)
        ld(cosc[:], c_cosc, "cosc")
        gsel = sb3("gsel", [2, 8], F32)
        ld(gsel[:], c_gsel, "gsel")
        lsel = sb3("lsel", [2, 2, 128], F32)
        ld(lsel[:], c_lsel, "lsel")
        iot = sb3("iot", [128, 1], F32)
        ld(iot[:], c_iota, "iot")
        ones128 = sb3("ones128", [128, 128], F32)
        S.op(V1, lambda e: e.memset(ones128[:], 1.0), writes=["ones128"])
        gck = sb3("gck", [128, 64], F32)
        ld(gck[:], nsa_k_norm[0].partition_broadcast(128), "gck")
        b2B = sb3("b2B", [128, 2, 2, 64], F32)
        for e_ in range(2):
            for g_ in range(2):
                ld(b2B[:, e_, g_, :], cmp_b2[e_].partition_broadcast(128), ("b2B", e_, g_))
        B2K = [("b2B", e_, g_) for e_ in range(2) for g_ in range(2)]
        PTB = sb3("PTB", [128, DB * 64], I32)
        ld(PTB[:], ptab.partition_broadcast(128), "PTB")
        S.barrier()
        IDX = sb3("IDX", [128, DB * 64], I32)
        ts(V1, IDX[:], PTB[:], 128.0, iot[:], ALU.mult, ALU.add, ["PTB", "iot"], ["IDX"])
        W1bd = sb3("W1bd", [128, 2, 32, 128], BF16)
        W2bd = sb3("W2bd", [128, 2, 128], F32)
        peT = sb3("peT", [128, 2, 32], BF16)
        b1T = sb3("b1T", [128, 2], F32)
        bias1 = sb3("bias1", [128, 2], F32)
        S.op(G1, lambda e: e.memset(W2bd[:], 0.0), writes=["W2bd"])
        for e_ in range(2):
            for g_ in range(2):
                S.dma("sync", lambda e, e_=e_, g_=g_: e.dma_start(
                    out=W2bd[g_ * 64:(g_ + 1) * 64, e_, g_ * 64:(g_ + 1) * 64], in_=cmp_w2[e_]),
                      ("W2bd", e_, g_), reads=[], writes=["W2bd"])
                S.dma("sync", lambda e, e_=e_, g_=g_: e.dma_start(
                    out=b1T[g_ * 64:(g_ + 1) * 64, e_:e_ + 1], in_=cmp_b1[e_].rearrange("(h o) -> h o", o=1),
                    allow_slow_non_contiguous=True),
                      ("b1T", e_, g_), writes=[("b1T", e_, g_)])
        B1K = [("b1T", e_, g_) for e_ in range(2) for g_ in range(2)]
        with nc.sbuf_tensor("w1st", [128, 32, 128], F32) as w1st, nc.sbuf_tensor("pest", [128, 32], F32) as pest:
            for e_ in range(2):
                S.op(G1, lambda e: e.memset(w1st[:], 0.0), writes=["w1st"])
                for g_ in range(2):
                    S.dma("sync", lambda e, e_=e_, g_=g_: e.dma_start(
                        out=w1st[g_ * 64:(g_ + 1) * 64, :, g_ * 64:(g_ + 1) * 64],
                        in_=cmp_w1[e_].rearrange("p d h -> d p h")),
                          ("w1st", g_), writes=["w1st"])
                    S.dma("sync", lambda e, e_=e_, g_=g_: e.dma_start(
                        out=pest[g_ * 64:(g_ + 1) * 64, :], in_=cmp_pe[e_].rearrange("p d -> d p"),
                        allow_slow_non_contiguous=True),
                          ("pest", g_), writes=["pest"])
                S.op(V1, lambda e, e_=e_: e.tensor_copy(out=W1bd[:, e_, :, :], in_=w1st[:]),
                     reads=["w1st"], writes=[("W1bd", e_)])
                S.op(V1, lambda e, e_=e_: e.tensor_copy(out=peT[:, e_, :], in_=pest[:]),
                     reads=["pest"], writes=[("peT", e_)])
            S.barrier()
        for e_ in range(2):
            for p_ in range(32):
                S.op("tensor", lambda e, e_=e_, p_=p_: e.matmul(psD[:, e_:e_ + 1], lhsT=W1bd[:, e_, p_, :],
                                                                rhs=peT[:, e_, p_:p_ + 1],
                                                                start=(p_ == 0), stop=(p_ == 31)),
                     reads=[("W1bd", e_), ("peT", e_)], writes=["psD"])
        tt(V1, bias1[:], psD[:, 0:2], b1T[:], ALU.add, B1K, ["psD", "bias1"])

        NG = 4
        gbuf = [sb3(f"gbuf{i}", [128, 512], F32) for i in range(NG)]
        XT = [sb3(f"XT{i}", [128, 8192], BF16) for i in range(2)]
        Vsel = sb3("Vsel", [128, 64, 130], BF16)
        S.op(G1, lambda e: e.memset(Vsel[:], 1.0), writes=["Vsel"])
        ssel = sb3("ssel", [128, 64, 8], F32)
        eselb = sb3("eselb", [128, 64, 8], BF16)
        prodS = sb3("prodS", [128, 8, 64], F32)
        QB = sb3("QB", [128, 768], F32)
        hiS = sb3("hiS", [128, 512], F32)
        pre = sb3("pre", [128, 512], F32)
        hidT = [sb3(f"hidT{i}", [128, 512], F32) for i in range(2)]
        for i in range(2):
            S.op(G1, lambda e, i=i: e.memset(hidT[i][:], 0.0), writes=[("hidT", i)])
        cktok = sb3("cktok", [128, 4, 2, 64], F32)
        cvtok = sb3("cvtok", [128, 4, 128], F32)
        ckss = sb3("ckss", [128, 8], F32)
        ckr = sb3("ckr", [128, 4, 2, 64], F32)
        cktmp = sb3("cktmp", [128, 4, 2, 32], F32)
        prodC = sb3("prodC", [128, 4, 8, 64], F32)
        sc = sb3("sc", [128, 4, 8], F32)
        rsum = sb3("rsum", [128, 8], F32)
        PG = sb3("PG", [128, 4, 2], F32)
        score = sb3("score", [2, 128], F32)
        score2 = sb3("score2", [2, 128], F32)
        m8 = sb3("m8", [2, 16], F32)
        maskt = sb3("maskt", [2, 128], F32)
        rhs_eo = sb3("rhs_eo", [2, 2, 64, 8], F32)
        Mt = sb3("Mt", [128, 2, 513], F32)
        S.op(G1, lambda e: e.memset(Mt[:], 1.0), writes=["Mt"])
        Wt = sb3("Wt", [128, 4, 257], F32)
        S.op(G1, lambda e: e.memset(Wt[:], 1.0), writes=["Wt"])
        sm = sb3("sm", [128, 2, 4], F32)
        sw = sb3("sw", [128, 4, 8], F32)
        resb = sb3("resb", [8, 4, 260], F32)
        Qq = QB[:, 0:512].rearrange("p (g j d) -> p g j d", g=2, j=4)

        for b_ in range(nseq):
            S.op("tensor", lambda e, b_=b_: e.matmul(psC[:, 0:512], lhsT=onehot[:, b_, :], rhs=QM[:, 0:512],
                                                     start=True, stop=True),
                 reads=[("QM", 0), "onehot"], writes=["psC"])
            S.op("tensor", lambda e, b_=b_: e.matmul(psT[:, 0:256], lhsT=onehot[:, b_, :], rhs=QM[:, 512:768],
                                                     start=True, stop=True),
                 reads=[("QM", 1), "onehot"], writes=["psT"])
            S.op(A1, lambda e: e.copy(out=QB[:, 0:512], in_=psC[:, 0:512]), writes=["psC", ("QB", 0)])
            S.op(A1, lambda e: e.copy(out=QB[:, 512:768], in_=psT[:, 0:256]), writes=["psT", ("QB", 1)])
            S.dma("sync", lambda e, b_=b_: e.dma_start(out=Mt[:, :, 0:512],
                                                       in_=cmem[b_].rearrange("(mt p) c -> p mt c", p=128)),
                  "Mt", writes=["Mt"])
            for mt in range(2):
                tt(V1, prodS[:, 0:4, :], Mt[:, mt, 0:256].rearrange("p (h d) -> p h d", h=4),
                   QB[:, 512:768].rearrange("p (h d) -> p h d", h=4), ALU.mult, ["Mt", ("QB", 1)], ["prodS"])
                S.op(V1, lambda e, mt=mt: e.tensor_reduce(out=sm[:, mt, :], in_=prodS[:, 0:4, :], axis=AX.X, op=ALU.add),
                     reads=["prodS"], writes=["sm"])
            act(sm[:], sm[:], AF.Exp, ["sm"], ["sm"], scale=SCALE)
            for mt in range(2):
                S.op("tensor", lambda e, mt=mt: e.matmul(psE[0:4, 0:257], lhsT=sm[:, mt, :], rhs=Mt[:, mt, 256:513],
                                                         start=(mt == 0), stop=(mt == 1)),
                     reads=["sm", "Mt"], writes=["psE"])
            S.op(V1, lambda e: e.tensor_copy(out=resb[0:4, 3, 0:257], in_=psE[0:4, 0:257]), writes=["psE", ("resb", 3)])
            S.dma(OUTQ, lambda e, b_=b_: e.dma_start(out=scr_m[b_], in_=resb[0:4, 3, 0:257]),
                  "scr_m", reads=[("resb", 3)], writes=[("scr_m", b_)])
            S.dma("sync", lambda e, b_=b_: e.dma_start(out=Wt[:, :, 0:256],
                                                       in_=cwin[b_].rearrange("(kt p) c -> p kt c", p=128)),
                  "Wt", writes=["Wt"])
            for kt in range(4):
                tt(V1 if kt % 2 == 0 else G1, prodC[:, kt, :, :].rearrange("p (g j) d -> p g j d", g=2),
                   Wt[:, kt, 0:128].rearrange("p (g d) -> p g d", g=2).unsqueeze(2).to_broadcast([128, 2, 4, 64]),
                   Qq, ALU.mult, ["Wt", ("QB", 0)], [("prodC", kt)])
            S.op(V1, lambda e: e.tensor_reduce(out=sw[:].rearrange("p a h -> p (a h)"),
                                               in_=prodC[:].rearrange("p a h d -> p (a h) d"), axis=AX.X, op=ALU.add),
                 reads=[("prodC", k) for k in range(4)], writes=["sw"])
            act(sw[:], sw[:], AF.Exp, ["sw"], ["sw"], scale=SCALE)
            tt(V1, sw[:], sw[:], cst[:, 4:8].unsqueeze(2).to_broadcast([128, 4, 8]), ALU.mult, ["sw", "cst"], ["sw"])
            for kt in range(4):
                S.op("tensor", lambda e, kt=kt: e.matmul(psE[0:8, 0:129], lhsT=sw[:, kt, :], rhs=Wt[:, kt, 128:257],
                                                         start=(kt == 0), stop=(kt == 3)),
                     reads=["sw", "Wt"], writes=["psE"])
            S.op(V1, lambda e: e.tensor_copy(out=resb[0:8, 2, 0:129], in_=psE[0:8, 0:129]), writes=["psE", ("resb", 2)])
            S.dma(OUTQ, lambda e, b_=b_: e.dma_start(out=scr_w[b_], in_=resb[0:8, 2, 0:129]),
                  "scr_w", reads=[("resb", 2)], writes=[("scr_w", b_)])
            for j in range(64):
                gi = j % NG
                n_ = b_ * 64 + j
                S.dma("gpsimd", lambda e, gi=gi, n_=n_: e.indirect_dma_start(
                    out=gbuf[gi][:], out_offset=None, in_=pool[:, :],
                    in_offset=bass.IndirectOffsetOnAxis(ap=IDX[:, n_:n_ + 1], axis=0)),
                      ("gbuf", gi), reads=["IDX"], writes=[("gbuf", gi)])
                q4 = j % 4
                S.op("tensor", lambda e, gi=gi, q4=q4: e.transpose(out=psA[0][:, q4 * 128:(q4 + 1) * 128],
                                                                   in_=gbuf[gi][:, 0:128], identity=identF[:]),
                     reads=[("gbuf", gi), "identF"], writes=[("psA", 0)])
                S.op("tensor", lambda e, gi=gi, q4=q4: e.transpose(out=psA[1][:, q4 * 128:(q4 + 1) * 128],
                                                                   in_=gbuf[gi][:, 128:256], identity=identF[:]),
                     reads=[("gbuf", gi), "identF"], writes=[("psA", 1)])
                if q4 == 3:
                    c0 = (j - 3) * 128
                    S.op(A1, lambda e, c0=c0: e.copy(out=XT[0][:, c0:c0 + 512], in_=psA[0][:, :]),
                         writes=[("psA", 0), ("XT", 0)])
                    S.op(V1, lambda e, c0=c0: e.tensor_copy(out=XT[1][:, c0:c0 + 512], in_=psA[1][:, :]),
                         writes=[("psA", 1), ("XT", 1)])
                tt(V1, prodS[:].rearrange("p (g j) d -> p g j d", g=2),
                   gbuf[gi][:, 256:384].rearrange("p (g d) -> p g d", g=2).unsqueeze(2).to_broadcast([128, 2, 4, 64]),
                   Qq, ALU.mult, [("gbuf", gi), ("QB", 0)], ["prodS"])
                S.op(V1, lambda e, j=j: e.tensor_reduce(out=ssel[:, j, :], in_=prodS[:], axis=AX.X, op=ALU.add),
                     reads=["prodS"], writes=["ssel"])
                S.op(G1, lambda e, gi=gi, j=j: e.tensor_copy(out=Vsel[:, j, 0:128], in_=gbuf[gi][:, 384:512]),
                     reads=[("gbuf", gi)], writes=["Vsel"])
            for e_ in range(2):
                for half in range(2):
                    pst = psB[half]
                    for p_ in range(16):
                        S.op("tensor", lambda e, e_=e_, half=half, p_=p_, pst=pst: e.matmul(
                            pst[:, 0:512], lhsT=W1bd[:, e_, half * 16 + p_, :],
                            rhs=XT[e_][:].rearrange("p (c r) -> p c r", r=16)[:, :, p_],
                            start=(p_ == 0), stop=(p_ == 15)),
                             reads=[("W1bd", e_), ("XT", e_)], writes=[("psB", half)])
                act(hiS[:], psB[1][:, :], AF.Identity, ["bias1"], [("psB", 1), "hiS"], bias=bias1[:, e_:e_ + 1])
                tt(V1, pre[:, 0:511], psB[0][:, 0:511], hiS[:, 1:512], ALU.add, ["hiS"], [("psB", 0), "pre"])
                act(hidT[e_][:, 0:511], pre[:, 0:511], AF.Silu, ["pre"], [("hidT", e_)])
            for cc in range(4):
                S.op("tensor", lambda e, cc=cc: e.matmul(psD[:, cc * 128:(cc + 1) * 128],
                                                         lhsT=hidT[0][:, cc * 128:(cc + 1) * 128], rhs=W2bd[:, 0, :],
                                                         start=True, stop=True),
                     reads=[("hidT", 0), "W2bd"], writes=["psD"])
                S.op("tensor", lambda e, cc=cc: e.matmul(psE[:, cc * 128:(cc + 1) * 128],
                                                         lhsT=hidT[1][:, cc * 128:(cc + 1) * 128], rhs=W2bd[:, 1, :],
                                                         start=True, stop=True),
                     reads=[("hidT", 1), "W2bd"], writes=["psE"])
            tt(V1, cktok[:].rearrange("p c g d -> p c (g d)"), psD[:, :].rearrange("p (c x) -> p c x", c=4),
               b2B[:, 0, :, :].rearrange("p g d -> p (g d)").unsqueeze(1).to_broadcast([128, 4, 128]), ALU.add,
               B2K, ["psD", "cktok"])
            tt(V1, cvtok[:], psE[:, :].rearrange("p (c x) -> p c x", c=4),
               b2B[:, 1, :, :].rearrange("p g d -> p (g d)").unsqueeze(1).to_broadcast([128, 4, 128]), ALU.add,
               B2K, ["psE", "cvtok"])
            ck8 = cktok[:].rearrange("p c g d -> p (c g) d")
            tt(G1, prodC[:, 0, :, :], ck8, ck8, ALU.mult, ["cktok"], [("prodC", 0)])
            S.op(V1, lambda e: e.tensor_reduce(out=ckss[:], in_=prodC[:, 0, :, :], axis=AX.X, op=ALU.add),
                 reads=[("prodC", 0)], writes=["ckss"])
            act(ckss[:], ckss[:], AF.Sqrt, ["ckss", "epsT"], ["ckss"], scale=1.0 / 64, bias=epsT[:])
            S.op(V1, lambda e: e.reciprocal(out=ckss[:], in_=ckss[:]), reads=["ckss"], writes=["ckss"])
            tt(V1, ck8, ck8, ckss[:].unsqueeze(2).to_broadcast([128, 8, 64]), ALU.mult, ["cktok", "ckss"], ["cktok"])
            tt(G1, ck8, ck8, gck[:].unsqueeze(1).to_broadcast([128, 8, 64]), ALU.mult, ["cktok", "gck"], ["cktok"])
            cosb = cosc[:, :, 0:32].unsqueeze(2).to_broadcast([128, 4, 2, 32])
            sinb = cosc[:, :, 32:64].unsqueeze(2).to_broadcast([128, 4, 2, 32])
            x1 = cktok[:, :, :, 0:32]
            x2 = cktok[:, :, :, 32:64]
            tt(V1, ckr[:, :, :, 0:32], x1, cosb, ALU.mult, ["cktok", "cosc"], [("ckr", 0)])
            tt(G1, cktmp[:], x2, sinb, ALU.mult, ["cktok", "cosc"], ["cktmp"])
            tt(V1, ckr[:, :, :, 0:32], ckr[:, :, :, 0:32], cktmp[:], ALU.subtract, [("ckr", 0), "cktmp"], [("ckr", 0)])
            tt(G1, ckr[:, :, :, 32:64], x2, cosb, ALU.mult, ["cktok", "cosc"], [("ckr", 1)])
            tt(V1, cktmp[:], x1, sinb, ALU.mult, ["cktok", "cosc", ("ckr", 0)], ["cktmp"])
            tt(V1, ckr[:, :, :, 32:64], ckr[:, :, :, 32:64], cktmp[:], ALU.add, [("ckr", 1), "cktmp"], [("ckr", 1)])
            for cc in range(4):
                tt(V1 if cc % 2 == 0 else G1, prodC[:, cc, :, :].rearrange("p (g j) d -> p g j d", g=2),
                   ckr[:, cc, :, :].unsqueeze(2).to_broadcast([128, 2, 4, 64]), Qq, ALU.mult,
                   [("ckr", 0), ("ckr", 1), ("QB", 0)], [("prodC", cc)])
            S.op(V1, lambda e: e.tensor_reduce(out=sc[:].rearrange("p a h -> p (a h)"),
                                               in_=prodC[:].rearrange("p a h d -> p (a h) d"), axis=AX.X, op=ALU.add),
                 reads=[("prodC", k) for k in range(4)], writes=["sc"])
            act(sc[:], sc[:], AF.Exp, ["sc"], ["sc"], scale=SCALE)
            tt(V1, sc[:], sc[:], cst[:, 0:4].unsqueeze(2).to_broadcast([128, 4, 8]), ALU.mult, ["sc", "cst"], ["sc"])
            for cc in range(4):
                S.op("tensor", lambda e, cc=cc: e.matmul(psC[:, 0:8], lhsT=ones128[:], rhs=sc[:, cc, :],
                                                         start=(cc == 0), stop=(cc == 3)),
                     reads=["sc", "ones128"], writes=["psC"])
            S.op(V1, lambda e: e.reciprocal(out=rsum[:], in_=psC[:, 0:8]), writes=["psC", "rsum"])
            tt(V1, sc[:], sc[:], rsum[:].unsqueeze(1).to_broadcast([128, 4, 8]), ALU.mult, ["sc", "rsum"], ["sc"])
            for cc in range(4):
                S.op("tensor", lambda e, cc=cc: e.matmul(psT[0:8, 0:128], lhsT=sc[:, cc, :], rhs=cvtok[:, cc, :],
                                                         start=(cc == 0), stop=(cc == 3)),
                     reads=["sc", "cvtok"], writes=["psT"])
            S.op(V1, lambda e: e.tensor_copy(out=resb[0:8, 0, 0:128], in_=psT[0:8, 0:128]), writes=["psT", ("resb", 0)])
            S.dma(OUTQ, lambda e, b_=b_: e.dma_start(out=scr_c[b_], in_=resb[0:8, 0, 0:128]),
                  "scr_c", reads=[("resb", 0)], writes=[("scr_c", b_)])
            S.op(V1, lambda e: e.tensor_reduce(out=PG[:].rearrange("p c g -> p (c g)"),
                                               in_=sc[:].rearrange("p c (g j) -> p (c g) j", g=2), axis=AX.X, op=ALU.add),
                 reads=["sc"], writes=["PG"])
            for cc in range(4):
                S.op("tensor", lambda e, cc=cc: e.matmul(psC[0:2, 128:256], lhsT=PG[:, cc, :], rhs=OV[:, cc, :],
                                                         start=(cc == 0), stop=(cc == 3)),
                     reads=["PG", "OV"], writes=["psC"])
            S.op(V1, lambda e: e.tensor_copy(out=score[:], in_=psC[0:2, 128:256]), writes=["psC", "score"])
            S.op(V1, lambda e: e.memset(score[:, 0:1], BIG), reads=["score"], writes=["score"])
            S.op(V1, lambda e: e.memset(score[:, 127:128], BIG), reads=["score"], writes=["score"])
            S.op(V1, lambda e: e.max(out=m8[:, 0:8], in_=score[:]), reads=["score"], writes=["m8"])
            S.op(V1, lambda e: e.match_replace(out=score2[:], in_to_replace=m8[:, 0:8], in_values=score[:],
                                               imm_value=-1.0e30),
                 reads=["score", "m8"], writes=["score2"])
            S.op(V1, lambda e: e.max(out=m8[:, 8:16], in_=score2[:]), reads=["score2", "m8"], writes=["m8"])
            ts(V1, maskt[:], score[:], m8[:, 14:15], None, ALU.is_ge, None, ["score", "m8"], ["maskt"])
            mv = maskt[:].rearrange("p (j t) -> p j t", t=2)
            for eo in range(2):
                tt(V1, rhs_eo[:, eo, :, :], mv[:, :, eo].unsqueeze(2).to_broadcast([2, 64, 8]),
                   gsel[:].unsqueeze(1).to_broadcast([2, 64, 8]), ALU.mult, ["maskt", "gsel"], [("rhs_eo", eo)])
            for eo in range(2):
                S.op("tensor", lambda e, eo=eo: e.matmul(psC[:, 0:512], lhsT=lsel[:, eo, :],
                                                         rhs=rhs_eo[:, eo, :, :].rearrange("p j h -> p (j h)"),
                                                         start=(eo == 0), stop=(eo == 1)),
                     reads=[("rhs_eo", eo), "lsel"], writes=["psC"])
            act(ssel[:], ssel[:], AF.Exp, ["ssel"], ["ssel"], scale=SCALE)
            tt(V1, eselb[:].rearrange("p j h -> p (j h)"), ssel[:].rearrange("p j h -> p (j h)"), psC[:, 0:512],
               ALU.mult, ["ssel"], ["psC", "eselb"])
            for j in range(64):
                S.op("tensor", lambda e, j=j: e.matmul(psT[0:8, 128:257], lhsT=eselb[:, j, :], rhs=Vsel[:, j, 0:129],
                                                       start=(j == 0), stop=(j == 63)),
                     reads=["eselb", "Vsel"], writes=["psT"])
            S.op(V1, lambda e: e.tensor_copy(out=resb[0:8, 1, 0:129], in_=psT[0:8, 128:257]), writes=["psT", ("resb", 1)])
            S.dma(OUTQ, lambda e, b_=b_: e.dma_start(out=scr_s[b_], in_=resb[0:8, 1, 0:129]),
                  "scr_s", reads=[("resb", 1)], writes=[("scr_s", b_)])
        S.barrier()
        p3.close()
        if stage < 6:
            return _finish(nc, S, stack, sb, o_yp, o_ys, o_wkv_p, None)
        Rc = sb("Rc", [16, 8, 128], F32)
        Rs = sb("Rs", [16, 8, 129], F32)
        Rw = sb("Rw", [16, 8, 129], F32)
        Rm = sb("Rm", [16, 4, 257], F32)
        for (t_, scr_, nm) in ((Rc, scr_c, "scr_c"), (Rs, scr_s, "scr_s"), (Rw, scr_w, "scr_w"), (Rm, scr_m, "scr_m")):
            S.dma("sync", lambda e, t_=t_, scr_=scr_: e.dma_start(out=t_[:], in_=scr_),
                  "ld_" + nm, reads=[(nm, b_) for b_ in range(nseq)], writes=[nm + "_sb"])
        Wob = sb("Wob", [128, 8, D], BF16)
        wost = [sb(f"wost{i}", [128, D], F32) for i in range(2)]
        for k in range(8):
            i = k % 2
            ld(wost[i][:], w_out[k * 128:(k + 1) * 128, :], ("wost", i), shared=False)
            S.op(G1 if k % 2 == 0 else V1, lambda e, k=k, i=i: e.tensor_copy(out=Wob[:, k, :], in_=wost[i][:]),
                 reads=[("wost", i)], writes=[("Wob", k)])
        ld(xs_sb[:], xs[:, :], "xs_sb")
        S.barrier()
        pn = sb("pn", [16, 2, 8], F32)
        pr4 = sb("pr4", [16, 8, 64], F32)
        Qn = QM[:, 0:512].rearrange("p (g j d) -> p g j d", g=2, j=4)
        for bi, c0 in enumerate((0, 256)):
            tt(V1, pr4[:].rearrange("p (g j) d -> p g j d", g=2), Qn,
               newkv[:, c0:c0 + 128].rearrange("p (g d) -> p g d", g=2).unsqueeze(2).to_broadcast([16, 2, 4, 64]),
               ALU.mult, [("QM", 0), ("newkv", 0), ("newkv", 1), "pr4"], ["pr4"])
            S.op(V1, lambda e, bi=bi: e.tensor_reduce(out=pn[:, bi, :], in_=pr4[:], axis=AX.X, op=ALU.add),
                 reads=["pr4"], writes=[("pn", bi)])
        act(pn[:], pn[:], AF.Exp, [("pn", 0), ("pn", 1)], [("pn", 0), ("pn", 1)], scale=SCALE)
        onsa = sb("onsa", [16, 8, 64], F32)
        obr = sb("obr", [16, 8, 64], F32)
        den = sb("den", [16, 8], F32)
        g3 = gts[:].rearrange("p (h t) -> p h t", t=3)
        for g_ in range(2):
            S.op(V1, lambda e, g_=g_: e.tensor_copy(out=obr[:, 4 * g_:4 * g_ + 4, :],
                                                    in_=Rc[:, 4 * g_:4 * g_ + 4, 64 * g_:64 * g_ + 64]),
                 reads=["scr_c_sb", "obr"], writes=["obr"])
        tt(V1, onsa[:], obr[:], g3[:, :, 0:1].to_broadcast([16, 8, 64]), ALU.mult, ["obr", "gts"], ["onsa"])
        for bi, (R_, rk, c0) in enumerate(((Rs, "scr_s_sb", 128), (Rw, "scr_w_sb", 384))):
            for g_ in range(2):
                hs_ = slice(4 * g_, 4 * g_ + 4)
                tt(V1, obr[:, hs_, :], pn[:, bi, hs_].unsqueeze(2).to_broadcast([16, 4, 64]),
                   newkv[:, c0 + 64 * g_:c0 + 64 * g_ + 64].unsqueeze(1).to_broadcast([16, 4, 64]), ALU.mult,
                   [("pn", bi), ("newkv", 0), ("newkv", 1), "obr", "onsa"], ["obr"])
                tt(V1, obr[:, hs_, :], obr[:, hs_, :], R_[:, hs_, 64 * g_:64 * g_ + 64], ALU.add, ["obr", rk], ["obr"])
            tt(V1, den[:], R_[:, :, 128], pn[:, bi, :], ALU.add, [rk, ("pn", bi), "den"], ["den"])
            S.op(V1, lambda e: e.reciprocal(out=den[:], in_=den[:]), reads=["den"], writes=["den"])
            tt(V1, obr[:], obr[:], den[:].unsqueeze(2).to_broadcast([16, 8, 64]), ALU.mult, ["obr", "den"], ["obr"])
            tt(V1, obr[:], obr[:], g3[:, :, bi + 1:bi + 2].to_broadcast([16, 8, 64]), ALU.mult, ["obr", "gts"], ["obr"])
            tt(V1, onsa[:], onsa[:], obr[:], ALU.add, ["onsa", "obr"], ["onsa"])
        tt(V1, cat[:, 0:512], onsa[:].rearrange("p h d -> p (h d)"), sz[:, 0:512], ALU.mult,
           ["onsa", ("sz", 0)], [("cat", 0)])
        tt(V1, cat[:, 512:768], cat[:, 512:768], sz[:, 512:768], ALU.mult, [("cat", 1), ("sz", 1)], [("cat", 1)])
        S.op(V1, lambda e: e.reciprocal(out=den[:, 0:4], in_=Rm[:, :, 256]), reads=["scr_m_sb", "den", "obr"], writes=["den"])
        for h in range(4):
            tt(V1, obr[:, h, :], Rm[:, h, 64 * h:64 * h + 64], den[:, h:h + 1].to_broadcast([16, 64]), ALU.mult,
               ["scr_m_sb", "den", "obr"], ["obr"])
        tt(V1, cat[:, 768:1024], obr[:, 0:4, :].rearrange("p h d -> p (h d)"), sz[:, 768:1024], ALU.mult,
           ["obr", ("sz", 2)], [("cat", 2)])
        catT = sb("catT", [128, 8, 16], BF16)
        for k in range(8):
            S.op("tensor", lambda e, k=k: e.transpose(out=psD[:, k * 16:(k + 1) * 16], in_=cat[:, k * 128:(k + 1) * 128],
                                                      identity=identF[0:16, 0:16]),
                 reads=[("cat", 0), ("cat", 1), ("cat", 2), "identF"], writes=["psD"])
        S.op(V1, lambda e: e.tensor_copy(out=catT[:].rearrange("p k b -> p (k b)"), in_=psD[:, 0:128]),
             writes=["psD", "catT"])
        ysb = sb("ysb", [16, D], F32)
        for half in range(2):
            for k in range(8):
                S.op("tensor", lambda e, k=k, half=half: e.matmul(psA[half][0:16, 0:512], lhsT=catT[:, k, :],
                                                                  rhs=Wob[:, k, half * 512:(half + 1) * 512],
                                                                  start=(k == 0), stop=(k == 7)),
                     reads=["catT"] + [("Wob", kk) for kk in range(8)], writes=[("psA", half)])
            tt(V1, ysb[:, half * 512:(half + 1) * 512], psA[half][0:16, 0:512], xs_sb[:, half * 512:(half + 1) * 512],
               ALU.add, ["xs_sb"], [("psA", half), ("ysb", half)])
        S.dma(OUTQ, lambda e: e.dma_start(out=o_ys[:, :], in_=ysb[:]), "st_ysb",
              reads=[("ysb", 0), ("ysb", 1)], writes=["o_ys"], is_out=True)
        return _finish(nc, S, stack, sb, o_yp, None, o_wkv_p, None)


def _finish(nc, S, stack, sb, o_yp, o_ys, o_wkv_p, o_wkv_s):
    if True:
        zt = sb("zt", [128, D], F32)
        S.op("gpsimd", lambda e: e.memset(zt[:], 0.0), writes=["zt"])
        for tt in range(NT if not (YP_ON and S.last_w.get(("o_yp", 0)) is not None) else 0):
            S.dma("sync", lambda e, tt=tt: e.dma_start(out=o_yp[tt * 128:(tt + 1) * 128, :], in_=zt[:]),
                  ("zfill", tt % 4), reads=["zt"], writes=[("o_yp", tt)], is_out=True)
        if o_ys is not None:
            S.dma("sync", lambda e: e.dma_start(out=o_ys[:, :], in_=zt[0:DB, :]),
                  ("zfill", 0), reads=["zt"], writes=["o_ys"], is_out=True)
        if not RWKV_ON:
          S.dma("sync", lambda e: e.dma_start(out=o_wkv_p.rearrange("h v k -> v h k"),
                                            in_=zt[0:64, 0:256].rearrange("p (h k) -> p h k", h=4)),
                ("zfill", 1), reads=["zt"], writes=["o_wkv_p"], is_out=True)
        if o_wkv_s is not None:
          S.dma("sync", lambda e: e.dma_start(out=o_wkv_s.rearrange("b h v k -> v (b h) k"),
                                            in_=zt[0:64, :].rearrange("p (a k) -> p a k", k=64)[:, 0:1, :].to_broadcast([64, DB * 4, 64])),
                ("zfill", 2), reads=["zt"], writes=["o_wkv_s"], is_out=True)

        S.finish("sync")
        with nc.Block() as block:
            S.run(block)
    return nc


def _rope_tables():
    half = 32
    inv = (10000.0 ** (-2.0 * np.arange(half, dtype=np.float32) / 64)).astype(np.float32)
    pos = np.arange(SEQ, dtype=np.float32)
    ang = pos[:, None] * inv[None, :]
    cos = np.cos(ang).astype(np.float32).reshape(NT, 128, half).transpose(1, 0, 2)
    sin = np.sin(ang).astype(np.float32).reshape(NT, 128, half).transpose(1, 0, 2)
    angd = np.float32(8192.0) * inv
    cosd = np.concatenate([np.cos(angd), np.sin(angd)]).astype(np.float32)
    return np.ascontiguousarray(cos), np.ascontiguousarray(sin), cosd


def _decode_consts():
    inv = (10000.0 ** (-2.0 * np.arange(32, dtype=np.float32) / 64)).astype(np.float32)
    c = np.arange(512)
    j = np.arange(128)
    ovl = np.clip(np.minimum(c[:, None] * 16 + 32, j[None, :] * 64 + 64)
                  - np.maximum(c[:, None] * 16, j[None, :] * 64), 0, None).astype(np.float32) / 32.0
    ovl[511, :] = 0.0
    ov = np.ascontiguousarray(ovl.reshape(4, 128, 128).transpose(1, 0, 2))
    ang = (c * 16 + 31).astype(np.float32)[:, None] * inv[None, :]
    cs = np.concatenate([np.cos(ang), np.sin(ang)], axis=1).astype(np.float32)
    cosc = np.ascontiguousarray(cs.reshape(4, 128, 64).transpose(1, 0, 2))
    valid = np.ones((128, 8), np.float32)
    valid[127, 3] = 0.0
    valid[0, 4] = 0.0
    gsel = np.zeros((2, 8), np.float32)
    gsel[0, 0:4] = 1.0
    gsel[1, 4:8] = 1.0
    lsel = np.zeros((2, 2, 128), np.float32)
    lsel[:, 0, 0:64] = 1.0
    lsel[:, 1, 64:128] = 1.0
    iota = np.arange(128, dtype=np.float32).reshape(128, 1)
    shift = np.zeros((128, 2, 128), np.float32)
    for t in range(1, 128):
        shift[t - 1, 0, t] = 1.0
    shift[127, 1, 0] = 1.0
    pp = np.arange(128)[:, None]
    ff = np.arange(128)[None, :]
    masks = np.zeros((128, 19, 128), np.float32)
    for k_ in range(17):
        masks[:, k_, :] = (16 * pp - ff <= 128 * k_ - 31)
    masks[:, 17, :] = (pp <= ff)
    masks[:, 18, :] = (pp > ff)
    return {"c_ov": ov, "c_cosc": cosc, "c_valid": valid, "c_gsel": gsel, "c_lsel": lsel, "c_iota": iota, "c_shift": shift, "c_masks": masks}


def _onehot():
    oh = np.zeros((16, 16, 128), np.float32)
    for b in range(16):
        oh[b, b, :] = 1.0
    return oh


_NC_CACHE = {}
STAGE = 99
import os as _os
SUB = int(_os.environ.get('SUB', '99'))


def kernel(x_prompt, x_sample, mem_prompt, cache_nsa, cache_win, cache_mem,
           state_rwkv_shift, state_rwkv_wkv, page_table,
           ln_g, w_in, nsa_q_norm, nsa_k_norm, cmp_pe, cmp_w1, cmp_b1, cmp_w2, cmp_b2,
           rwkv_mu, rwkv_w0, rwkv_w2, rwkv_a0, rwkv_a2, rwkv_k_k, rwkv_k_a, rwkv_r_k,
           rwkv_ln_w, rwkv_ln_b, mem_norm_g, w_mem_kv, mem_q_norm, mem_k_norm, w_out):
    f = lambda a: np.ascontiguousarray(np.asarray(a, dtype=np.float32))
    cos, sin, cosd = _rope_tables()
    ident = np.eye(128, dtype=np.float32)
    if "nc" not in _NC_CACHE:
        _NC_CACHE["nc"] = build_nc(STAGE)
    nc = _NC_CACHE["nc"]
    shared = dict(_decode_consts())
    shared.update({
        "ln_g": f(ln_g[0]), "w_in": f(w_in[0]), "nsa_k_norm": f(nsa_k_norm[0]),
        "mem_norm_g": f(mem_norm_g[0]), "w_mem_kv": f(w_mem_kv[0]), "mem_k_norm": f(mem_k_norm[0]),
        "rwkv_mu": f(rwkv_mu[0]), "rwkv_w0": f(rwkv_w0[0]), "rwkv_w2": f(rwkv_w2[0]),
        "rwkv_a0": f(rwkv_a0[0]), "rwkv_a2": f(rwkv_a2[0]), "rwkv_k_k": f(rwkv_k_k[0]),
        "rwkv_k_a": f(rwkv_k_a[0]), "rwkv_r_k": f(rwkv_r_k[0]).reshape(256),
        "rwkv_ln_w": f(rwkv_ln_w[0]), "rwkv_ln_b": f(rwkv_ln_b[0]),
        "nsa_q_norm": f(nsa_q_norm[0]), "mem_q_norm": f(mem_q_norm[0]), "w_out": f(w_out[0]),
        "c_onehot": _onehot(),
        "cmp_pe": f(cmp_pe[0]), "cmp_w1": f(cmp_w1[0]), "cmp_b1": f(cmp_b1[0]),
        "cmp_w2": f(cmp_w2[0]), "cmp_b2": f(cmp_b2[0]),
        "pool": f(cache_nsa[0]).reshape(-1, 512),
        "c_cos": cos, "c_sin": sin, "c_cosd": cosd, "c_ident": ident,
    })
    x_prompt = np.asarray(x_prompt)
    in_maps = []
    for c in range(NCORES):
        sl = slice(c * DB, (c + 1) * DB)
        m = dict(shared)
        m["xp"] = f(x_prompt[c])
        m["xs"] = f(np.asarray(x_sample)[sl, 0, :])
        m["memp"] = f(np.asarray(mem_prompt)[c])
        m["cwin"] = f(np.asarray(cache_win)[0, sl].reshape(DB, 512, 256))
        m["st_shift"] = f(np.asarray(state_rwkv_shift)[0, sl])
        m["st_wkv"] = f(np.asarray(state_rwkv_wkv)[0, sl])
        m["cmem"] = f(np.asarray(cache_mem)[0, sl].reshape(DB, 256, 512))
        m["ptab"] = np.ascontiguousarray(np.asarray(page_table)[sl].reshape(-1).astype(np.int32))
        in_maps.append(m)
    res = run_bass_kernel_spmd(nc, in_maps, core_ids=list(range(NCORES)))
    R = res.results

    def cat(name, shape_per_core, axis0_stack=True):
        return np.stack([np.asarray(R[c][name], dtype=np.float32).reshape(shape_per_core) for c in range(NCORES)])

    y_prompt = cat("o_yp", (SEQ, D))
    y_sample = cat("o_ys", (DB, 1, D)).reshape(NCORES * DB, 1, D)
    nsa_rows_prompt = cat("o_nsa_p", (SEQ, 4, 2, 64))[None]
    win_prompt = cat("o_win_p", (512, 2, 2, 64))[None]
    shift_prompt = cat("o_shift_p", (896,))[None]
    wkv_prompt = cat("o_wkv_p", (4, 64, 64))[None]
    mem_kv_prompt = cat("o_memkv", (256, 2, 4, 64))[None]
    nsa_rows_sample = cat("o_nsa_s", (DB, 1, 4, 2, 64)).reshape(1, NCORES * DB, 1, 4, 2, 64)
    win_sample = cat("o_win_s", (DB, 512, 2, 2, 64)).reshape(1, NCORES * DB, 512, 2, 2, 64)
    shift_sample = cat("o_shift_s", (DB, 896)).reshape(1, NCORES * DB, 896)
    wkv_sample = cat("o_wkv_s", (DB, 4, 64, 64)).reshape(1, NCORES * DB, 4, 64, 64)
    return (y_prompt, y_sample, nsa_rows_prompt, win_prompt, shift_prompt, wkv_prompt,
            mem_kv_prompt, nsa_rows_sample, win_sample, shift_sample, wkv_sample)
```
